# Optimizing a Trainium2 kernel written in Bass

```python
import jax, jax.numpy as jnp
from jax import lax
import numpy as np

D_MODEL = 1024
BATCH = 8
SEQ = 4096
DEPTH = 4

N_HEADS = 8
HEAD_DIM = 64
KV_RANK = 256
IDX_HEADS = 8
IDX_DIM = 64
TOPK_MAX = 256
Q_BLOCK = 128
POOL_WINDOWS = (2, 4, 8, 16)
POOL_WIDTH = 512
POOL_GROUP = POOL_WIDTH // len(POOL_WINDOWS)
SSD_HEADS = 16
SSD_HEAD_DIM = 64
D_INNER = SSD_HEADS * SSD_HEAD_DIM
N_GROUPS = 2
D_STATE = 128
CONV_WIDTH = 4
CONV_DIM = D_INNER + 2 * N_GROUPS * D_STATE
CHUNK = 128
D_FF = 2816
PLE_DIM = 256
N_BRANCHES = 3
EPS = 1e-6
IN_SPLITS = (N_HEADS * HEAD_DIM, KV_RANK, IDX_HEADS * IDX_DIM, IDX_HEADS, IDX_DIM,
             POOL_WIDTH, D_INNER, CONV_DIM, SSD_HEADS, N_BRANCHES * D_MODEL)
IN_COLS = sum(IN_SPLITS)

kernel_name = "hybrid_dsa_pool_ssd_macaron_trunk"


def rmsnorm(x, g):
    x32 = x.astype(jnp.float32)
    y = x32 * lax.rsqrt(jnp.mean(x32 * x32, axis=-1, keepdims=True) + EPS)
    return (y * g.astype(jnp.float32)).astype(x.dtype)


def swiglu(x, w_gate, w_up, w_down):
    return (jax.nn.silu(x @ w_gate) * (x @ w_up)) @ w_down


def dsa_attention(q, c_kv, q_idx, w_idx, k_idx, w_uk, w_uv):
    b, s = q.shape[:2]
    n_blk = s // Q_BLOCK
    k_top = min(TOPK_MAX, s // 4)
    q_lat = jnp.einsum("bshd,rhd->bshr", q, w_uk)

    def to_blocks(a):
        return jnp.moveaxis(a.reshape((b, n_blk, Q_BLOCK) + a.shape[2:]), 1, 0)

    key_pos = jnp.arange(s)
    gather = jax.vmap(lambda c, i: c[i])

    def one_block(args):
        j, ql, qi, wi = args
        q_pos = j * Q_BLOCK + jnp.arange(Q_BLOCK)
        causal = key_pos[None, :] <= q_pos[:, None]
        idx = jnp.einsum("bqh,bqhs->bqs", wi,
                         jax.nn.relu(jnp.einsum("bqhd,bsd->bqhs", qi, k_idx)))
        idx = jnp.where(causal[None], idx.astype(jnp.float32), -jnp.inf)
        _, sel = lax.top_k(idx, k_top)
        valid = sel <= q_pos[None, :, None]
        c_sel = gather(c_kv, sel)
        logits = jnp.einsum("bqhr,bqkr->bqhk", ql, c_sel).astype(jnp.float32) * (HEAD_DIM ** -0.5)
        logits = jnp.where(valid[:, :, None, :], logits, -jnp.inf)
        probs = jax.nn.softmax(logits, axis=-1).astype(c_sel.dtype)
        return jnp.einsum("bqhk,bqkr->bqhr", probs, c_sel)

    o_lat = lax.map(one_block, (jnp.arange(n_blk), to_blocks(q_lat), to_blocks(q_idx), to_blocks(w_idx)))
    o_lat = jnp.moveaxis(o_lat, 0, 1).reshape(b, s, N_HEADS, KV_RANK)
    o = jnp.einsum("bshr,rhd->bshd", o_lat, w_uv)
    return o.reshape(b, s, N_HEADS * HEAD_DIM)


def pool_mixer(xp, w_pool, scale):
    b, s, _ = xp.shape
    x32 = xp.astype(jnp.float32)
    cs = jnp.concatenate([jnp.zeros((b, 1, POOL_WIDTH), jnp.float32), jnp.cumsum(x32, axis=1)], axis=1)
    pos = jnp.arange(s)
    outs = []
    for g, w in enumerate(POOL_WINDOWS):
        sl = slice(g * POOL_GROUP, (g + 1) * POOL_GROUP)
        lo = jnp.maximum(pos + 1 - w, 0)
        win_sum = cs[:, 1:, sl] - cs[:, lo, sl]
        count = jnp.minimum(pos + 1, w).astype(jnp.float32)[None, :, None]
        outs.append(win_sum / count - x32[:, :, sl])
    pooled = jnp.stack(outs, axis=2).astype(xp.dtype)
    mixed = jnp.einsum("bsgc,gcd->bsgd", pooled, w_pool).reshape(b, s, POOL_WIDTH)
    return mixed * scale


def causal_depthwise_conv(x, w, bias):
    k = w.shape[0]
    y = lax.conv_general_dilated(x, w[:, None, :], window_strides=(1,), padding=[(k - 1, 0)],
                                 dimension_numbers=("NWC", "WIO", "NWC"),
                                 feature_group_count=x.shape[-1])
    return y + bias


def segsum(a):
    t = a.shape[-1]
    rep = jnp.broadcast_to(a[..., None], a.shape + (t,))
    rep = jnp.where(jnp.tril(jnp.ones((t, t), bool), -1), rep, 0.0)
    ss = jnp.cumsum(rep, axis=-2)
    return jnp.where(jnp.tril(jnp.ones((t, t), bool), 0), ss, -jnp.inf)


def ssd_chunked(xdt, da, bm, cm):
    b, s, h, p = xdt.shape
    g, n = bm.shape[2], bm.shape[3]
    e = h // g
    c = s // CHUNK
    dt_ = xdt.dtype
    X = xdt.reshape(b, c, CHUNK, g, e, p)
    A = da.reshape(b, c, CHUNK, g, e).transpose(0, 3, 4, 1, 2)
    Bc = bm.reshape(b, c, CHUNK, g, n)
    Cc = cm.reshape(b, c, CHUNK, g, n)
    a_cs = jnp.cumsum(A, axis=-1)
    L = jnp.exp(segsum(A)).astype(dt_)
    cb = jnp.einsum("bclgn,bcsgn->bgcls", Cc, Bc)
    y_diag = jnp.einsum("bgecls,bcsgep->bclgep", cb[:, :, None] * L, X)
    decay_states = jnp.exp(a_cs[..., -1:] - a_cs).astype(dt_)
    states = jnp.einsum("bclgn,bgecl,bclgep->bcgepn", Bc, decay_states, X)
    chunk_decay = jnp.exp(a_cs[..., -1]).astype(dt_)

    def step(carry, inp):
        st, dec = inp
        return carry * dec[..., None, None] + st, carry

    init = jnp.zeros((b, g, e, p, n), dt_)
    _, prev = lax.scan(step, init, (jnp.moveaxis(states, 1, 0), jnp.moveaxis(chunk_decay, 3, 0)))
    prev = jnp.moveaxis(prev, 0, 1)
    y_off = jnp.einsum("bclgn,bcgepn,bgecl->bclgep", Cc, prev, jnp.exp(a_cs).astype(dt_))
    return (y_diag + y_off).reshape(b, s, h, p)


def ssd_mixer(z, xbc, dt_raw, conv_w, conv_b, dt_bias, a_log, d_skip, norm_g):
    b, s, _ = z.shape
    xbc = jax.nn.silu(causal_depthwise_conv(xbc, conv_w, conv_b))
    xs, bm, cm = jnp.split(xbc, [D_INNER, D_INNER + N_GROUPS * D_STATE], axis=-1)
    xs = xs.reshape(b, s, SSD_HEADS, SSD_HEAD_DIM)
    bm = bm.reshape(b, s, N_GROUPS, D_STATE)
    cm = cm.reshape(b, s, N_GROUPS, D_STATE)
    dt = jax.nn.softplus((dt_raw + dt_bias).astype(jnp.float32))
    da = dt * (-jnp.exp(a_log.astype(jnp.float32)))
    y = ssd_chunked(xs * dt[..., None].astype(xs.dtype), da, bm, cm)
    y = y + xs * d_skip[:, None]
    y = y.reshape(b, s, D_INNER) * jax.nn.silu(z)
    return rmsnorm(y, norm_g)


def setup_inputs(seed: int = 0) -> dict:
    key = jax.random.key(seed)
    ks = iter(jax.random.split(key, 40))

    def w(shape, fan_in):
        return jax.random.normal(next(ks), shape, jnp.float32) * (fan_in ** -0.5)

    def gain(shape):
        return 1.0 + 0.02 * jax.random.normal(next(ks), shape, jnp.float32)

    L = DEPTH
    x = jax.random.normal(next(ks), (BATCH, SEQ, D_MODEL), jnp.float32)
    p = jax.random.normal(next(ks), (DEPTH, BATCH, SEQ, PLE_DIM), jnp.float32)
    dt0 = jnp.exp(jax.random.uniform(next(ks), (L, SSD_HEADS), jnp.float32,
                                     jnp.log(1e-3), jnp.log(1e-1)))
    dt_bias = dt0 + jnp.log(-jnp.expm1(-dt0))
    a_log = jnp.log(jax.random.uniform(next(ks), (L, SSD_HEADS), jnp.float32, 1.0, 16.0))
    return {
        "x": x,
        "p": p,
        "ffn1_norm": gain((L, D_MODEL)),
        "ffn1_w_gate": w((L, D_MODEL, D_FF), D_MODEL),
        "ffn1_w_up": w((L, D_MODEL, D_FF), D_MODEL),
        "ffn1_w_down": w((L, D_FF, D_MODEL), D_FF),
        "mix_norm": gain((L, D_MODEL)),
        "w_in": w((L, D_MODEL, IN_COLS), D_MODEL),
        "kv_norm": gain((L, KV_RANK)),
        "idx_k_norm": gain((L, IDX_DIM)),
        "w_uk": w((L, KV_RANK, N_HEADS, HEAD_DIM), KV_RANK),
        "w_uv": w((L, KV_RANK, N_HEADS, HEAD_DIM), KV_RANK),
        "pool_w": w((L, len(POOL_WINDOWS), POOL_GROUP, POOL_GROUP), POOL_GROUP),
        "pool_scale": gain((L, POOL_WIDTH)),
        "conv_w": w((L, CONV_WIDTH, CONV_DIM), CONV_WIDTH),
        "conv_b": 0.01 * jax.random.normal(next(ks), (L, CONV_DIM), jnp.float32),
        "dt_bias": dt_bias,
        "a_log": a_log,
        "d_skip": gain((L, SSD_HEADS)),
        "ssd_norm": gain((L, D_INNER)),
        "w_br_attn": w((L, N_HEADS * HEAD_DIM, D_MODEL), N_HEADS * HEAD_DIM),
        "w_br_pool": w((L, POOL_WIDTH, D_MODEL), POOL_WIDTH),
        "w_br_ssd": w((L, D_INNER, D_MODEL), D_INNER),
        "w_out": w((L, D_MODEL, D_MODEL), D_MODEL),
        "ffn2_norm": gain((L, D_MODEL)),
        "ffn2_w_gate": w((L, D_MODEL, D_FF), D_MODEL),
        "ffn2_w_up": w((L, D_MODEL, D_FF), D_MODEL),
        "ffn2_w_down": w((L, D_FF, D_MODEL), D_FF),
        "ple_norm": gain((L, D_MODEL)),
        "ple_w_gate": w((L, D_MODEL, D_MODEL), D_MODEL),
        "ple_w_proj": w((L, PLE_DIM, D_MODEL), PLE_DIM),
        "final_norm": gain((D_MODEL,)),
    }


def reference(x, p, ffn1_norm, ffn1_w_gate, ffn1_w_up, ffn1_w_down, mix_norm, w_in,
              kv_norm, idx_k_norm, w_uk, w_uv, pool_w, pool_scale, conv_w, conv_b,
              dt_bias, a_log, d_skip, ssd_norm, w_br_attn, w_br_pool, w_br_ssd, w_out,
              ffn2_norm, ffn2_w_gate, ffn2_w_up, ffn2_w_down, ple_norm, ple_w_gate,
              ple_w_proj, final_norm):
    b, s, _ = x.shape
    offs = [int(v) for v in np.cumsum(IN_SPLITS)[:-1]]
    h = x
    for i in range(DEPTH):
        h = h + 0.5 * swiglu(rmsnorm(h, ffn1_norm[i]), ffn1_w_gate[i], ffn1_w_up[i], ffn1_w_down[i])
        u = rmsnorm(h, mix_norm[i])
        (q, c_kv, q_idx, w_idx, k_idx, x_pool, z, xbc, dt_raw, gate_raw) = jnp.split(u @ w_in[i], offs, axis=-1)
        y_attn = dsa_attention(q.reshape(b, s, N_HEADS, HEAD_DIM), rmsnorm(c_kv, kv_norm[i]),
                               q_idx.reshape(b, s, IDX_HEADS, IDX_DIM), w_idx,
                               rmsnorm(k_idx, idx_k_norm[i]), w_uk[i], w_uv[i])
        y_pool = pool_mixer(x_pool, pool_w[i], pool_scale[i])
        y_ssd = ssd_mixer(z, xbc, dt_raw, conv_w[i], conv_b[i], dt_bias[i], a_log[i], d_skip[i], ssd_norm[i])
        g_attn, g_pool, g_ssd = jnp.split(jax.nn.sigmoid(gate_raw), N_BRANCHES, axis=-1)
        merged = (g_attn * (y_attn @ w_br_attn[i]) + g_pool * (y_pool @ w_br_pool[i])
                  + g_ssd * (y_ssd @ w_br_ssd[i]))
        h = h + merged @ w_out[i]
        h = h + 0.5 * swiglu(rmsnorm(h, ffn2_norm[i]), ffn2_w_gate[i], ffn2_w_up[i], ffn2_w_down[i])
        h = h + jax.nn.sigmoid(rmsnorm(h, ple_norm[i]) @ ple_w_gate[i]) * (p[i] @ ple_w_proj[i])
    return rmsnorm(h, final_norm)
```

```python
import numpy as np
import concourse.bass as bass
import concourse.mybir as mybir
from concourse.bass_utils import run_bass_kernel_spmd

F32 = mybir.dt.float32
BF16 = mybir.dt.bfloat16
I32 = mybir.dt.int32
I8 = mybir.dt.int8
ALU = mybir.AluOpType
AF = mybir.ActivationFunctionType
AX = mybir.AxisListType

D = 1024
S_FULL = 4096
L_FULL = 4
DFF = 2816
NFF = DFF // 128
TT = 512
EPS = 1e-6
NEG = -1.0e30
O_Q, O_CKV, O_QI, O_WI, O_KI, O_XP, O_Z, O_XBC, O_DT, O_GATE = 0, 512, 768, 1280, 1288, 1352, 1864, 2888, 4424, 4440
IN_COLS = 7512
NBIS = 22

_DT_SIZE = {F32: 4, BF16: 2, I32: 4, I8: 1}


def _dsize(dt):
    for k, v in _DT_SIZE.items():
        if dt == k:
            return v
    return 4


class Op:
    __slots__ = ("eng", "fn", "idx", "is_dma", "grp", "waits")


class Sched:
    G = 256
    ENGS = ("pe", "dve", "act", "pool", "sp")

    def __init__(self):
        self.streams = {e: [] for e in self.ENGS}
        self.ncomp = {e: 0 for e in self.ENGS}
        self.last_w = {}
        self.readers = {}
        self.grp_count = {}
        self.seen = {e: {} for e in self.ENGS}

    def keys_of(self, x):
        if isinstance(x, (str, tuple)):
            return [x]
        sp = str(x.space)
        if "DRAM" in sp:
            raise ValueError("DRAM AP needs manual key")
        pairs = list(x.ap)
        pitch = pairs[0][0]
        esz = _dsize(x.dtype)
        lo = x.offset % pitch if pitch > 0 else x.offset
        ext = 0
        for st, cnt in pairs[1:]:
            ext += abs(st) * (cnt - 1)
        hi = lo + ext + 1
        lo_b, hi_b = lo * esz, hi * esz
        name = x.tensor.name
        return [(name, g) for g in range(lo_b // self.G, (hi_b - 1) // self.G + 1)]

    def add(self, eng, fn, reads=(), writes=(), dma=None):
        op = Op()
        op.eng, op.fn, op.is_dma, op.grp = eng, fn, dma is not None, dma
        rk = [k for r in reads for k in self.keys_of(r)]
        wk = [k for w in writes for k in self.keys_of(w)]
        deps = set()
        for k in rk:
            o = self.last_w.get(k)
            if o is not None:
                deps.add(o)
        for k in wk:
            o = self.last_w.get(k)
            if o is not None:
                deps.add(o)
            for o in self.readers.get(k, ()):
                deps.add(o)
        deps.discard(op)
        waits = {}
        for d in deps:
            if d.is_dma:
                sem = "g_" + d.grp
                val = self.grp_count[d.grp]
            else:
                if d.eng == eng and not op.is_dma and eng == "pe":
                    continue
                sem = "e_" + d.eng
                val = d.idx
            if waits.get(sem, 0) < val:
                waits[sem] = val
        seen = self.seen[eng]
        op.waits = []
        for sem, val in waits.items():
            if seen.get(sem, 0) < val:
                seen[sem] = val
                op.waits.append((sem, val))
        if op.is_dma:
            self.grp_count[dma] = self.grp_count.get(dma, 0) + 16
            op.idx = self.grp_count[dma]
        else:
            self.ncomp[eng] += 1
            op.idx = self.ncomp[eng]
        for k in wk:
            self.last_w[k] = op
            self.readers[k] = []
        for k in rk:
            self.readers.setdefault(k, []).append(op)
        self.streams[eng].append(op)
        return op

    def emit(self, nc, final_waits):
        names = ["e_" + e for e in self.ENGS] + ["g_" + g for g in self.grp_count]
        sems = {}
        import contextlib
        with contextlib.ExitStack() as es:
            for n in names:
                sems[n] = es.enter_context(nc.semaphore(n))
            block = es.enter_context(nc.Block())

            def run(engname):
                def body(eng):
                    for op in self.streams[engname]:
                        for sem, val in op.waits:
                            eng.wait_ge(sems[sem], val)
                        ins = op.fn(eng)
                        if op.is_dma:
                            ins.then_inc(sems["g_" + op.grp], 16)
                        else:
                            ins.then_inc(sems["e_" + engname], 1)
                    if engname == "sp":
                        for g in final_waits:
                            eng.wait_ge(sems["g_" + g], self.grp_count[g])
                return body

            block.tensor(run("pe"))
            block.vector(run("dve"))
            block.scalar(run("act"))
            block.gpsimd(run("pool"))
            block.sync(run("sp"))


class Builder:
    def __init__(self, NT, NL, mix=7):
        self.NT, self.NL, self.mix = NT, NL, mix
        self.S = NT * TT
        self.nc = bass.Bass("TRN2", target_bir_lowering=False)
        self.sc = Sched()
        self.psrr = 0
        self.wrr = 0
        self.rr = {}

    def op(self, eng, fn, reads, writes):
        return self.sc.add(eng, fn, reads, writes)

    def dma(self, q, out, in_, reads, writes, grp, **kw):
        return self.sc.add(q, lambda e: e.dma_start(out=out, in_=in_, **kw), reads, writes, dma=grp)

    def mm(self, out, lhsT, rhs, start=True, stop=True, extra_reads=(), **kw):
        return self.op("pe", lambda e: e.matmul(out, lhsT=lhsT, rhs=rhs, start=start, stop=stop, **kw),
                       [lhsT, rhs] + list(extra_reads), [out])

    def tr(self, out, in_, ident):
        return self.op("pe", lambda e: e.transpose(out, in_, ident), [in_, ident], [out])

    def act(self, out, in_, func, bias=None, scale=None, eng="act", extra_reads=(), accum_out=None):
        kw = {}
        rd = [in_] + list(extra_reads)
        wr_extra = []
        if accum_out is not None:
            kw["accum_out"] = accum_out
            wr_extra.append(accum_out)
        if bias is not None:
            kw["bias"] = bias
            if not isinstance(bias, (int, float)):
                rd.append(bias)
        if scale is not None:
            kw["scale"] = scale
            if not isinstance(scale, (int, float)):
                rd.append(scale)
        return self.op(eng, lambda e: e.activation(out=out, in_=in_, func=func, **kw), rd, [out] + wr_extra)

    def ts(self, out, in0, s1, op0, s2=None, op1=None, eng="dve", accum_out=None):
        rd = [in0]
        for s in (s1, s2):
            if s is not None and not isinstance(s, (int, float)):
                rd.append(s)
        kw = {}
        if op1 is not None:
            kw["op1"] = op1
        wr = [out]
        if accum_out is not None:
            kw["accum_out"] = accum_out
            wr.append(accum_out)
        return self.op(eng, lambda e: e.tensor_scalar(out=out, in0=in0, scalar1=s1, scalar2=s2, op0=op0, **kw), rd, wr)

    def tt(self, out, in0, in1, op, eng="dve"):
        return self.op(eng, lambda e: e.tensor_tensor(out=out, in0=in0, in1=in1, op=op), [in0, in1], [out])

    def stt(self, out, in0, scalar, in1, op0, op1):
        rd = [in0, in1]
        if not isinstance(scalar, (int, float)):
            rd.append(scalar)
        return self.op("dve", lambda e: e.scalar_tensor_tensor(out=out, in0=in0, scalar=scalar, in1=in1, op0=op0, op1=op1), rd, [out])

    def copy(self, out, in_, eng="dve"):
        if eng == "act":
            return self.op("act", lambda e: e.copy(out=out, in_=in_), [in_], [out])
        return self.op(eng, lambda e: e.tensor_copy(out=out, in_=in_), [in_], [out])

    def memset(self, ap, val, eng="dve"):
        return self.op(eng, lambda e: e.memset(ap, val), [], [ap])

    def sb(self, name, shape, dt):
        return self.es.enter_context(self.nc.sbuf_tensor(name, list(shape), dt))

    def bank(self, pool="A"):
        banks = {"A": (0, 1, 2, 3), "B": (4, 5, 6, 7), "A2": (0, 1)}[pool]
        i = self.rr.get(pool, 0)
        self.rr[pool] = i + 1
        return banks[i % len(banks)]

    def rot(self, name, n):
        i = self.rr.get(name, 0)
        self.rr[name] = i + 1
        return i % n

    def wslot(self):
        i = self.wrr
        self.wrr += 1
        return i % self.NW

    def wload(self, parts):
        s = self.wslot()
        slot = self.wbuf[s]
        off = 0
        views = []
        for src in parts:
            shp = list(src.shape)
            n = 1
            for v in shp[1:]:
                n *= v
            dst = slot[0:shp[0], off:off + n]
            if len(shp) == 3:
                dst = dst.rearrange("p (c n) -> p c n", c=shp[1])
            self.dma("pool", dst, src, [], [dst], "w%d" % s)
            views.append(dst)
            off += n
        assert off <= self.WSZ, off
        return views

    def build(self):
        import contextlib
        nc = self.nc
        NT, NL, S = self.NT, self.NL, self.S
        with contextlib.ExitStack() as es:
            self.es = es
            dt_in = lambda name, shape: nc.dram_tensor(name, list(shape), F32, kind="ExternalInput").ap()
            self.x = dt_in("x", [S, D])
            self.p = dt_in("p", [L_FULL, S, 256])
            W = {}
            for name, shape in [
                ("ffn1_norm", [L_FULL, D]), ("ffn1_w_gate", [L_FULL, D, DFF]), ("ffn1_w_up", [L_FULL, D, DFF]),
                ("ffn1_w_down", [L_FULL, DFF, D]), ("mix_norm", [L_FULL, D]), ("w_in", [L_FULL, D, IN_COLS]),
                ("kv_norm", [L_FULL, 256]), ("idx_k_norm", [L_FULL, 64]), ("w_uk", [L_FULL, 256, 512]),
                ("w_uv", [L_FULL, 256, 512]), ("pool_w", [L_FULL, 4, 128, 128]), ("pool_scale", [L_FULL, 512]),
                ("conv_w", [L_FULL, 4, 1536]), ("conv_b", [L_FULL, 1536]), ("dt_bias", [L_FULL, 16]),
                ("a_log", [L_FULL, 16]), ("d_skip", [L_FULL, 16]), ("ssd_norm", [L_FULL, D]),
                ("w_br_attn", [L_FULL, 512, D]), ("w_br_pool", [L_FULL, 512, D]), ("w_br_ssd", [L_FULL, D, D]),
                ("w_out", [L_FULL, D, D]), ("ffn2_norm", [L_FULL, D]), ("ffn2_w_gate", [L_FULL, D, DFF]),
                ("ffn2_w_up", [L_FULL, D, DFF]), ("ffn2_w_down", [L_FULL, DFF, D]), ("ple_norm", [L_FULL, D]),
                ("ple_w_gate", [L_FULL, D, D]), ("ple_w_proj", [L_FULL, 256, D]), ("final_norm", [D]),
            ]:
                W[name] = dt_in(name, shape)
            self.W = W
            self.out = nc.dram_tensor("out", [S, D], F32, kind="ExternalOutput").ap()
            self.kc = nc.dram_tensor("kcache", [L_FULL, 512, S], BF16, kind="Internal").ap()
            self.vc = nc.dram_tensor("vcache", [L_FULL, S, 576], BF16, kind="Internal").ap()
            self.ic = nc.dram_tensor("icache", [L_FULL, 128, S], BF16, kind="Internal").ap()

            self.alloc()
            self.init_consts()
            for i in range(NT):
                self.load_x_tile(i)
                for l in range(NL):
                    self.ffn(i, l, 1)
                    self.mixer(i, l)
                    self.ffn(i, l, 2)
                    self.ple(i, l)
                self.final(i)
            self.sc.emit(nc, ["out"])
        return nc

    def alloc(self):
        sb = self.sb
        nc = self.nc
        self.ps = self.es.enter_context(nc.psum_tensor("ps", [128, 8, 512], F32))
        self.NW, self.WSZ = 5, 4096
        self.wbuf = [sb("wbuf%d" % i, [128, self.WSZ], BF16) for i in range(self.NW)]
        self.hT = sb("hT", [128, 8, TT], F32)
        self.xnT = sb("xnT", [128, 8, TT], BF16)
        AR = 48992 + 672
        self.arena = sb("arena", [128, AR], BF16)
        self.ar_off = 0

        def carve(nel, dt, shape=None, at=None):
            esz = _dsize(dt)
            if at is None:
                at = self.ar_off
            assert at % 4 == 0
            nb = nel * esz
            assert at + nb <= AR * 2, (at, nb)
            v = self.arena[:, at // 2: (at + nb) // 2]
            if dt != BF16:
                v = v.bitcast(dt)
            self.ar_off = at + nb
            return v
        self.carve = carve
        K = 1024
        self.actT = carve(NFF * TT, BF16, at=0).rearrange("p (c n) -> p c n", c=NFF)
        self.m = carve(8 * TT, F32, at=0).rearrange("p (c n) -> p c n", c=8)
        self.xin = [carve(D, F32, at=22 * K + j * 4 * K) for j in range(2)]
        self.qz = carve(8 * 128, BF16, at=20 * K).rearrange("p (h n) -> p h n", h=8)
        BASE = 22 * K
        o = BASE
        self.idx_acc = carve(4096, F32, at=o)
        o += 16 * K
        self.masks = [carve(4096, BF16, at=o + j * 8 * K) for j in range(2)]; o += 16 * K
        self.junk = carve(4096, I8, at=o); o += 4 * K
        self.kidx_sb = carve(4096, BF16, at=o); o += 8 * K
        self.ckv_f = carve(2 * TT, F32, at=o).rearrange("p (c n) -> p c n", c=2)
        self.kt_out = carve(4 * TT, BF16, at=o + 4 * K).rearrange("p (c n) -> p c n", c=4)
        self.v_out = carve(4 * 576, BF16, at=o + 8 * K).rearrange("p (c n) -> p c n", c=4)
        self.kst = [carve(4 * 512, BF16, at=o + j * 4 * K).rearrange("p (c n) -> p c n", c=4) for j in range(2)]; o += 8 * K
        self.vst = [carve(4 * 576, BF16, at=o + j * 4608).rearrange("p (c n) -> p c n", c=4) for j in range(2)]; o += 2 * 4608
        self.pTt = [carve(8 * 128, BF16, at=o + j * 2 * K).rearrange("p (c n) -> p c n", c=8) for j in range(2)]; o += 4 * K
        self.rtmp = [carve(512, F32, at=o + j * 2 * K) for j in range(2)]; o += 4 * K
        self.maskT = [carve(128, BF16, at=o + j * 256) for j in range(2)]; o += 512
        self.o_sb = carve(512, BF16, at=o); o += K
        self.y_attnT = carve(4 * TT, BF16, at=o).rearrange("p (c n) -> p c n", c=4); o += 4 * K
        self.att_end = o
        o = BASE
        self.xpT = carve(4 * 528, F32, at=o).rearrange("p (c n) -> p c n", c=4); o += 4 * 528 * 4
        self.stmp = [carve(528, F32, at=o + j * 2112) for j in range(2)]; o += 2 * 2112
        self.pooledT = carve(4 * TT, BF16, at=o).rearrange("p (c n) -> p c n", c=4); o += 4 * K
        self.y_poolT = carve(4 * TT, BF16, at=o).rearrange("p (c n) -> p c n", c=4); o += 4 * K
        self.tmp16 = carve(16, F32, at=o); o += 64
        o = BASE
        self.zs = carve(8 * TT, BF16, at=o).rearrange("p (c n) -> p c n", c=8); o += 8 * K
        self.xsT = carve(8 * TT, F32, at=o).rearrange("p (c n) -> p c n", c=8); o += 16 * K
        self.BT = carve(2 * TT, BF16, at=o).rearrange("p (c n) -> p c n", c=2); o += 2 * K
        self.CT = carve(2 * TT, BF16, at=o).rearrange("p (c n) -> p c n", c=2); o += 2 * K
        self.dtT = carve(TT, F32, at=o); o += 2 * K
        self.daT = carve(TT, F32, at=o); o += 2 * K
        self.acsT = carve(TT, F32, at=o); o += 2 * K
        self.decT = carve(TT, F32, at=o); o += 2 * K
        self.eaT = carve(TT, F32, at=o); o += 2 * K
        self.cacc = [carve(TT, F32, at=o + j * 2 * K) for j in range(2)]; o += 4 * K
        self.Xb = carve(1024, BF16, at=o).rearrange("p (h e) -> p h e", h=16); o += 2 * K
        self.Xd = carve(1024, BF16, at=o).rearrange("p (h e) -> p h e", h=16); o += 2 * K
        self.tok = carve(64, F32, at=o); o += 256
        self.etot = carve(16, F32, at=o); o += 64
        self.dg = carve(16, F32, at=o); o += 64
        self.yt1 = carve(1024, F32, at=o).rearrange("p (h e) -> p h e", h=16); o += 4 * K
        self.ytok = carve(1024, F32, at=o); o += 4 * K
        self.Sb = carve(1024, BF16, at=o); o += 2 * K
        self.Btok = carve(256, BF16, at=o).rearrange("p (c n) -> p c n", c=2); o += 512
        self.CBm = carve(256, F32, at=o).rearrange("p (c n) -> p c n", c=2); o += K
        self.A_all = carve(2048, F32, at=o).rearrange("p (h n) -> p h n", h=16); o += 8 * K
        self.Mh_all = carve(2048, BF16, at=o).rearrange("p (h n) -> p h n", h=16); o += 4 * K
        self.stmp2 = carve(512, F32, at=o); o += 2 * K
        self.ssd_end = o
        assert max(self.att_end, self.ssd_end) <= AR * 2, (self.att_end, self.ssd_end)
        self.mergedT = carve(8 * TT, BF16, at=BASE).rearrange("p (c n) -> p c n", c=8)
        self.qT = carve(4 * TT, BF16, at=16 * K).rearrange("p (c n) -> p c n", c=4)
        self.qiT = sb("qiT", [128, 4, TT], BF16)
        self.ckvn = sb("ckvn", [128, 2, TT], BF16)
        self.kin = sb("kin", [128, 1, TT], BF16)
        self.wiT = sb("wiT", [8, TT], F32)
        self.w_tok = sb("w_tok", [128, 4, 8], F32)
        self.S_f = sb("S_f", [128, L_FULL, 1024], F32)
        self.chalo = sb("chalo", [128, L_FULL, 12, 3], F32)
        self.phalo = sb("phalo", [128, L_FULL, 4, 16], F32)
        self.bis = sb("bis", [128, 8, 64], F32)
        self.hks = sb("hks", [128, NBIS + 1], F32)
        self.pw2 = sb("pw2", [128, NBIS + 1], F32)
        self.thr = sb("thr", [128, 1], F32)
        self.rcp = sb("rcp", [128, 8], F32)
        self.neg_tri = sb("neg_tri", [128, 128], F32)
        self.tri_ml = sb("tri_ml", [128, 128], F32)
        self.mgt = sb("mgt", [128, 128], F32)
        self.ones_f = sb("ones_f", [128, 128], F32)
        self.one_t = sb("one_t", [128, 1], F32)
        self.invcnt = sb("invcnt", [128, 4, 16], F32)
        self.g_kv = sb("g_kv", [128, L_FULL, 2], F32)
        self.g_ki = sb("g_ki", [128, L_FULL], F32)
        self.pscale = sb("pscale", [128, L_FULL, 4], F32)
        self.convw = sb("convw", [128, L_FULL, 4, 12], F32)
        self.convb = sb("convb", [128, L_FULL, 12], F32)
        self.dtb = sb("dtb", [16, L_FULL], F32)
        self.negA = sb("negA", [16, L_FULL], F32)
        self.dexp = sb("dexp", [128, L_FULL, 8], F32)
        self.g_ssd = sb("g_ssd", [128, L_FULL, 8], F32)
        self.ident_f = sb("ident_f", [128, 128], F32)
        self.ident_b = sb("ident_b", [128, 128], BF16)
        self.ones_b = sb("ones_b", [128, 128], BF16)
        self.iot = sb("iot", [128, 128], I32)
        self.sq = [sb("sq%d" % i, [128, TT], BF16) for i in range(2)]
        self.rstd = sb("rstd", [128, TT], F32)
        self.f32t = [sb("f32t%d" % i, [128, TT], F32) for i in range(2)]
        self.ps_b = self.ps[:].bitcast(BF16)
        self.pT = sb("pTin", [128, 2, TT], BF16)
        self.pleg = self.rstd
        self.g_ffn1 = sb("g_ffn1", [128, L_FULL, 8], F32)
        self.g_mix = sb("g_mix", [128, L_FULL, 8], F32)
        self.g_ffn2 = sb("g_ffn2", [128, L_FULL, 8], F32)
        self.g_ple = sb("g_ple", [128, L_FULL, 8], F32)
        self.g_fin = sb("g_fin", [128, 8], F32)
        self.eps_t = sb("eps_t", [128, 1], F32)

    def init_consts(self):
        nc = self.nc
        self.op("pool", lambda e: e.iota(self.iot[:], pattern=[[1, 128]], base=0, channel_multiplier=-1), [], [self.iot[:]])
        self.ts(self.ident_f[:], self.iot[:], 0.0, ALU.is_equal)
        self.ts(self.ident_b[:], self.iot[:], 0.0, ALU.is_equal)
        self.memset(self.ones_b[:], 1.0)
        self.memset(self.eps_t[:], EPS)
        self.memset(self.ones_f[:], 1.0)
        self.memset(self.one_t[:], 1.0)
        self.ts(self.neg_tri[:], self.iot[:], 0.0, ALU.is_gt, s2=NEG, op1=ALU.mult)
        self.ts(self.tri_ml[:], self.iot[:], 0.0, ALU.is_ge)
        self.ts(self.mgt[:], self.iot[:], 0.0, ALU.is_lt)
        for k in range(NBIS + 1):
            self.memset(self.pw2[:, k:k + 1], 2.0 ** (-k))
        for g in range(4):
            wg_ = 2 ** (g + 1)
            self.memset(self.invcnt[:, g, :], 1.0 / wg_)
            for t in range(wg_ - 1):
                self.memset(self.invcnt[:, g, t:t + 1], 1.0 / (t + 1))
        self.memset(self.S_f[:], 0.0)
        self.memset(self.chalo[:], 0.0)
        self.memset(self.phalo[:], 0.0)
        Wd_ = self.W
        sm = lambda t, src: self.dma("sp", t, src, [], [t], "init", allow_slow_non_contiguous=True)
        sm(self.g_kv[:], Wd_["kv_norm"].rearrange("l (c p) -> p l c", p=128))
        sm(self.g_ki[0:64, :], Wd_["idx_k_norm"].rearrange("l p -> p l"))
        sm(self.g_ki[64:128, :], Wd_["idx_k_norm"].rearrange("l p -> p l"))
        sm(self.pscale[:], Wd_["pool_scale"].rearrange("l (c p) -> p l c", p=128))
        for l_ in range(L_FULL):
            sm(self.convw[:, l_], Wd_["conv_w"][l_].rearrange("k (c p) -> p k c", p=128))
        sm(self.convb[:], Wd_["conv_b"].rearrange("l (c p) -> p l c", p=128))
        sm(self.dtb[:], Wd_["dt_bias"].rearrange("l h -> h l"))
        sm(self.negA[:], Wd_["a_log"].rearrange("l h -> h l"))
        self.act(self.negA[:], self.negA[:], AF.Exp)
        self.ts(self.negA[:], self.negA[:], -1.0, ALU.mult)
        for half in range(2):
            sm(self.dexp[half * 64:(half + 1) * 64], Wd_["d_skip"][:, half::2].partition_broadcast(64))
        sm(self.g_ssd[:], Wd_["ssd_norm"].rearrange("l (c p) -> p l c", p=128))
        W = self.W
        for t, name in [(self.g_ffn1, "ffn1_norm"), (self.g_mix, "mix_norm"), (self.g_ffn2, "ffn2_norm"), (self.g_ple, "ple_norm")]:
            src = W[name].rearrange("l (c p) -> p l c", p=128)
            self.dma("sp", t[:], src, [], [t[:]], "init", allow_slow_non_contiguous=True)
        self.dma("sp", self.g_fin[:], W["final_norm"].rearrange("(c p) -> p c", p=128), [], [self.g_fin[:]], "init",
                 allow_slow_non_contiguous=True)

    def load_x_tile(self, i):
        for j in range(4):
            xb = self.xin[self.rot("xin", 2)]
            r0 = i * TT + j * 128
            self.dma("sp", xb[:], self.x[r0:r0 + 128, :], [], [xb[:]], "xin")
            for half in range(2):
                b = self.bank("A")
                for c in range(4):
                    cc = half * 4 + c
                    self.tr(self.ps[:, b, c * 128:(c + 1) * 128], xb[:, cc * 128:(cc + 1) * 128], self.ident_f[:])
                dst = self.hT[:, half * 4:half * 4 + 4, j * 128:(j + 1) * 128]
                src = self.ps[:, b, :].rearrange("p (c n) -> p c n", c=4)
                self.copy(dst, src, eng="act" if half else "dve")

    def rmsnorm_T(self, src, nch, gcol, dst, dfeat, Kp=128, N=TT):
        b = self.bank("A")
        acc = self.ps[:, b, 0:N]
        for c in range(nch):
            sq = self.sq[self.rot("sq", 2)]
            self.act(sq[0:Kp, 0:N], src[:, c, :], AF.Square)
            self.mm(acc, self.ones_b[0:Kp, :], sq[0:Kp, 0:N], start=(c == 0), stop=(c == nch - 1))
        self.act(self.rstd[:, 0:N], acc, AF.Sqrt, bias=self.eps_t[:, 0:1], scale=1.0 / dfeat)
        self.op("dve", lambda e: e.reciprocal(out=self.rstd[:, 0:N], in_=self.rstd[:, 0:N]), [self.rstd[:, 0:N]], [self.rstd[:, 0:N]])
        for c in range(nch):
            self.stt(dst[:, c, :], src[:, c, :], gcol(c), self.rstd[0:Kp, 0:N], ALU.mult, ALU.mult)

    def ffn(self, i, l, which):
        W = self.W
        pre = "ffn%d_" % which
        g = self.g_ffn1 if which == 1 else self.g_ffn2
        self.rmsnorm_T(self.hT[:, :, :], 8, lambda c: g[:, l, c:c + 1], self.xnT[:, :, :], D)
        wg_v = W[pre + "w_gate"][l].rearrange("(c p) n -> p c n", p=128)
        wu_v = W[pre + "w_up"][l].rearrange("(c p) n -> p c n", p=128)
        wd_v = W[pre + "w_down"][l].rearrange("(c p) n -> p c n", p=128)
        for c0 in range(0, DFF, 512):
            nco = min(512, DFF - c0)
            (wg,) = self.wload([wg_v[:, :, c0:c0 + nco]])
            (wu,) = self.wload([wu_v[:, :, c0:c0 + nco]])
            for j in range(nco // 128):
                f = c0 // 128 + j
                bg = self.bank("A")
                bu = self.bank("A")
                for k in range(8):
                    self.mm(self.ps[:, bg, :], wg[:, k, j * 128:(j + 1) * 128], self.xnT[:, k, :], start=(k == 0), stop=(k == 7))
                for k in range(8):
                    self.mm(self.ps[:, bu, :], wu[:, k, j * 128:(j + 1) * 128], self.xnT[:, k, :], start=(k == 0), stop=(k == 7))
                sg = self.f32t[self.rot("f32t", 2)]
                self.act(sg[:], self.ps[:, bg, :], AF.Silu)
                self.tt(self.actT[:, f, :], sg[:], self.ps[:, bu, :], ALU.mult)
        for half in range(2):
            for f0 in range(0, NFF, 4):
                nf = min(4, NFF - f0)
                (wd,) = self.wload([wd_v[:, f0:f0 + nf, half * 512:(half + 1) * 512]])
                for r in range(nf):
                    f = f0 + r
                    for dc in range(4):
                        self.mm(self.ps[:, 4 + dc, :], wd[:, r, dc * 128:(dc + 1) * 128], self.actT[:, f, :],
                                start=(f == 0), stop=(f == NFF - 1))
            for dc in range(4):
                hc = self.hT[:, half * 4 + dc, :]
                self.stt(hc, self.ps[:, 4 + dc, :], 0.5, hc, ALU.mult, ALU.add)

    def mixer(self, i, l):
        if not self.mix:
            return
        W = self.W
        g = self.g_mix
        self.rmsnorm_T(self.hT[:, :, :], 8, lambda c: g[:, l, c:c + 1], self.xnT[:, :, :], D)
        self.win = W["w_in"][l].rearrange("(c p) n -> p c n", p=128)
        first = True
        if self.mix & 2:
            self.pool_branch(i, l)
            self.lift(l, 1, self.y_poolT, 4, "w_br_pool", first)
            first = False
        if self.mix & 4:
            self.ssd_branch(i, l)
            self.lift(l, 2, self.zs, 8, "w_br_ssd", first)
            first = False
        if self.mix & 1:
            self.att_branch(i, l)
            self.lift(l, 0, self.y_attnT, 4, "w_br_attn", first)
            first = False
        for c in range(8):
            self.copy(self.mergedT[:, c, :], self.m[:, c, :], eng="act" if c % 2 else "dve")
        wo_v = W["w_out"][l].rearrange("(c p) n -> p c n", p=128)
        for half in range(2):
            (wo,) = self.wload([wo_v[:, :, half * 512:(half + 1) * 512]])
            for dc in range(4):
                b = self.bank("A")
                for k in range(8):
                    self.mm(self.ps[:, b, :], wo[:, k, dc * 128:(dc + 1) * 128], self.mergedT[:, k, :], start=(k == 0), stop=(k == 7))
                hc = self.hT[:, half * 4 + dc, :]
                self.tt(hc, hc, self.ps[:, b, :], ALU.add)

    def lift(self, l, b, yT, nk, wname, first):
        wbr = self.W[wname][l].rearrange("(c p) n -> p c n", p=128)
        for half in range(2):
            (wb,) = self.wload([wbr[:, :, half * 512:(half + 1) * 512]])
            g0 = O_GATE + b * 1024 + half * 512
            (wgt,) = self.wload([self.win[:, :, g0:g0 + 512]])
            for dc in range(4):
                bb = self.bank("A")
                for k in range(nk):
                    self.mm(self.ps[:, bb, :], wb[:, k, dc * 128:(dc + 1) * 128], yT[:, k, :], start=(k == 0), stop=(k == nk - 1))
                bg = self.bank("A")
                for k in range(8):
                    self.mm(self.ps[:, bg, :], wgt[:, k, dc * 128:(dc + 1) * 128], self.xnT[:, k, :], start=(k == 0), stop=(k == 7))
                self.act(self.pleg[:], self.ps[:, bg, :], AF.Sigmoid)
                mc = self.m[:, half * 4 + dc, :]
                if first:
                    self.tt(mc, self.pleg[:], self.ps[:, bb, :], ALU.mult)
                else:
                    tmp = self.f32t[self.rot("f32t", 2)]
                    self.tt(tmp[:], self.pleg[:], self.ps[:, bb, :], ALU.mult)
                    self.tt(mc, mc, tmp[:], ALU.add)

    def pool_branch(self, i, l):
        W = self.W
        (wxp,) = self.wload([self.win[:, :, O_XP:O_XP + 512]])
        (wpool,) = self.wload([W["pool_w"][l].rearrange("g c d -> c g d")])
        for g in range(4):
            b = self.bank("A")
            for k in range(8):
                self.mm(self.ps[:, b, :], wxp[:, k, g * 128:(g + 1) * 128], self.xnT[:, k, :], start=(k == 0), stop=(k == 7))
            xp = self.xpT[:, g, :]
            self.copy(xp[:, 16:528], self.ps[:, b, :], eng="act")
            self.copy(xp[:, 0:16], self.phalo[:, l, g, :])
            self.copy(self.phalo[:, l, g, :], xp[:, 512:528])
            cur = xp
            a = 0
            for step in range(g + 1):
                sh = 2 ** step
                a += sh
                nxt = self.stmp[self.rot("stmp", 2)]
                self.tt(nxt[:, a:528], cur[:, a:528], cur[:, a - sh:528 - sh], ALU.add)
                cur = nxt
            wg_ = 2 ** (g + 1)
            self.stt(self.pooledT[:, g, :], cur[:, 16:528], 1.0 / wg_, xp[:, 16:528], ALU.mult, ALU.subtract)
            if i == 0:
                self.tt(self.tmp16[:], cur[:, 16:32], self.invcnt[:, g, :], ALU.mult)
                self.tt(self.pooledT[:, g, 0:16], self.tmp16[:], xp[:, 16:32], ALU.subtract)
            b2 = self.bank("A")
            self.mm(self.ps[:, b2, :], wpool[:, g, :], self.pooledT[:, g, :])
            self.ts(self.y_poolT[:, g, :], self.ps[:, b2, :], self.pscale[:, l, g:g + 1], ALU.mult)

    def ssd_branch(self, i, l):
        W = self.W
        for half in range(2):
            (wz,) = self.wload([self.win[:, :, O_Z + half * 512:O_Z + (half + 1) * 512]])
            for j in range(4):
                c = half * 4 + j
                b = self.bank("A")
                for k in range(8):
                    self.mm(self.ps[:, b, :], wz[:, k, j * 128:(j + 1) * 128], self.xnT[:, k, :], start=(k == 0), stop=(k == 7))
                self.act(self.zs[:, c, :], self.ps[:, b, :], AF.Silu)
        for third in range(3):
            (wx,) = self.wload([self.win[:, :, O_XBC + third * 512:O_XBC + (third + 1) * 512]])
            for j in range(4):
                c = third * 4 + j
                b = self.bank("A")
                for k in range(8):
                    self.mm(self.ps[:, b, :], wx[:, k, j * 128:(j + 1) * 128], self.xnT[:, k, :], start=(k == 0), stop=(k == 7))
                psb = self.ps[:, b, :]
                acc = self.cacc[self.rot("cacc", 2)]
                cw = lambda kk: self.convw[:, l, kk, c:c + 1]
                hal = self.chalo[:, l, c, :]
                self.act(acc[:], psb, AF.Identity, scale=cw(3), bias=self.convb[:, l, c:c + 1])
                self.stt(acc[:, 1:512], psb[:, 0:511], cw(2), acc[:, 1:512], ALU.mult, ALU.add)
                self.stt(acc[:, 2:512], psb[:, 0:510], cw(1), acc[:, 2:512], ALU.mult, ALU.add)
                self.stt(acc[:, 3:512], psb[:, 0:509], cw(0), acc[:, 3:512], ALU.mult, ALU.add)
                self.stt(acc[:, 0:1], hal[:, 2:3], cw(2), acc[:, 0:1], ALU.mult, ALU.add)
                self.stt(acc[:, 0:2], hal[:, 1:3], cw(1), acc[:, 0:2], ALU.mult, ALU.add)
                self.stt(acc[:, 0:3], hal[:, 0:3], cw(0), acc[:, 0:3], ALU.mult, ALU.add)
                self.copy(hal, psb[:, 509:512])
                if c < 8:
                    dst = self.xsT[:, c, :]
                elif c < 10:
                    dst = self.BT[:, c - 8, :]
                else:
                    dst = self.CT[:, c - 10, :]
                self.act(dst, acc[:], AF.Silu)
        (wdt,) = self.wload([self.win[:, :, O_DT:O_DT + 16]])
        b = self.bank("A")
        for k in range(8):
            self.mm(self.ps[0:16, b, :], wdt[:, k, :], self.xnT[:, k, :], start=(k == 0), stop=(k == 7))
        dtT, daT, acsT, decT, eaT = (t[0:16, :] for t in (self.dtT, self.daT, self.acsT, self.decT, self.eaT))
        self.act(dtT, self.ps[0:16, b, :], AF.Exp, bias=self.dtb[:, l:l + 1])
        self.act(dtT, dtT, AF.Ln, bias=self.one_t[0:16, :])
        self.ts(daT, dtT, self.negA[:, l:l + 1], ALU.mult)
        for ch in range(4):
            sl = slice(ch * 128, (ch + 1) * 128)
            o_, d0, d1 = acsT[:, sl], self.ones_f[0:16, :], daT[:, sl]
            self.op("dve", lambda e, o_=o_, d0=d0, d1=d1: e.tensor_tensor_scan(out=o_, data0=d0, data1=d1, initial=0.0, op0=ALU.mult, op1=ALU.add),
                    [d0, d1], [o_])
            self.act(decT[:, sl], acsT[:, sl], AF.Exp, scale=-1.0, bias=acsT[:, ch * 128 + 127:ch * 128 + 128])
        self.act(eaT, acsT, AF.Exp)
        Sf = self.S_f[:, l, :]
        self.copy(self.Sb[:], Sf)
        for ch in range(4):
            sl = slice(ch * 128, (ch + 1) * 128)
            b = self.bank("A")
            for q_, src in enumerate((dtT, daT, decT, eaT)):
                self.tr(self.ps[:, b, q_ * 16:(q_ + 1) * 16], src[:, sl], self.ident_f[0:16, 0:16])
            self.copy(self.tok[:], self.ps[:, b, 0:64])
            self.ts(self.dg[0:16, :], self.ident_f[0:16, 0:16], eaT[:, ch * 128 + 127:ch * 128 + 128], ALU.mult)
            b = self.bank("A")
            self.mm(self.ps[:, b, 0:16], self.ones_f[0:16, :], self.dg[0:16, :])
            self.copy(self.etot[:], self.ps[:, b, 0:16])
            for half in range(2):
                for c in range(4):
                    self.tr(self.ps[:, 4 + half, c * 128:(c + 1) * 128], self.xsT[:, half * 4 + c, sl], self.ident_f[:])
            xs_tok = self.ps[:, 4:6, :].rearrange("p b (h e) -> p (b h) e", e=64)
            self.tt(self.Xb[:], xs_tok, self.tok[:, 0:16].unsqueeze(2).broadcast_to([128, 16, 64]), ALU.mult)
            self.tt(self.Xd[:], self.Xb[:], self.tok[:, 32:48].unsqueeze(2).broadcast_to([128, 16, 64]), ALU.mult)
            b = self.bank("A")
            for g_ in range(2):
                self.tr(self.ps_b[:, b, g_ * 128:(g_ + 1) * 128], self.BT[:, g_, sl], self.ident_b[:])
            self.copy(self.Btok[:], self.ps_b[:, b, 0:256].rearrange("p (c n) -> p c n", c=2), eng="act")
            for g_ in range(2):
                b = self.bank("A")
                self.mm(self.ps[:, b, 0:128], self.BT[:, g_, sl], self.CT[:, g_, sl])
                self.tt(self.CBm[:, g_, :], self.ps[:, b, 0:128], self.tri_ml[:], ALU.mult)
            self.tt(self.A_all[:], self.mgt[:].unsqueeze(1).broadcast_to([128, 16, 128]),
                    self.tok[:, 16:32].unsqueeze(2).broadcast_to([128, 16, 128]), ALU.mult)
            for h in range(16):
                self.mm(self.ps[:, h // 4, (h % 4) * 128:(h % 4 + 1) * 128], self.A_all[:, h, :], self.tri_ml[:], skip_group_check=True)
            for bk in range(4):
                self.act(self.Mh_all[:, 4 * bk:4 * bk + 4, :], self.ps[:, bk, :].rearrange("p (h n) -> p h n", h=4), AF.Exp)
            for bk in range(4):
                self.tt(self.Mh_all[:, 4 * bk:4 * bk + 4, :], self.Mh_all[:, 4 * bk:4 * bk + 4, :],
                        self.CBm[:, bk // 2, :].unsqueeze(1).broadcast_to([128, 4, 128]), ALU.mult)
            for h in range(16):
                self.mm(self.ps[:, 6 + h // 8, (h % 8) * 64:(h % 8 + 1) * 64], self.Mh_all[:, h, :], self.Xb[:, h, :],
                        start=(h % 8 == 0), stop=True, skip_group_check=True)
            for g_ in range(2):
                self.mm(self.ps[:, 4 + g_, :], self.CT[:, g_, sl], self.Sb[:, g_ * 512:(g_ + 1) * 512])
            yo = self.ps[:, 4:6, :].rearrange("p b (h e) -> p (b h) e", e=64)
            self.tt(self.yt1[:], yo, self.tok[:, 48:64].unsqueeze(2).broadcast_to([128, 16, 64]), ALU.mult)
            yd = self.ps[:, 6:8, :].rearrange("p b n -> p (b n)")
            self.tt(self.ytok[:], self.yt1[:].rearrange("p h e -> p (h e)"), yd, ALU.add)
            for g_ in range(2):
                self.mm(self.ps[:, 6 + g_, :], self.Btok[:, g_, :], self.Xd[:, g_ * 8:(g_ + 1) * 8, :].rearrange("p h e -> p (h e)"))
            Sf3 = Sf.rearrange("p (h e) -> p h e", e=64)
            self.tt(Sf3, Sf3, self.etot[:].unsqueeze(2).broadcast_to([128, 16, 64]), ALU.mult)
            self.tt(Sf, Sf, self.ps[:, 6:8, :].rearrange("p b n -> p (b n)"), ALU.add)
            self.copy(self.Sb[:], Sf)
            for half in range(2):
                for c in range(4):
                    cc = half * 4 + c
                    self.tr(self.ps[:, 4 + half, c * 128:(c + 1) * 128], self.ytok[:, cc * 128:(cc + 1) * 128], self.ident_f[:])
                for c in range(4):
                    cc = half * 4 + c
                    xc = self.xsT[:, cc, sl]
                    self.stt(xc, xc, self.dexp[:, l, cc:cc + 1], self.ps[:, 4 + half, c * 128:(c + 1) * 128], ALU.mult, ALU.add)
        for c in range(8):
            self.tt(self.xsT[:, c, :], self.xsT[:, c, :], self.zs[:, c, :], ALU.mult)
        self.rmsnorm_T(self.xsT[:, :, :], 8, lambda c: self.g_ssd[:, l, c:c + 1], self.zs[:, :, :], D)

    def att_branch(self, i, l):
        W = self.W
        win = self.win
        t0 = i * TT
        (wqi,) = self.wload([win[:, :, O_QI:O_QI + 512]])
        for c in range(4):
            b = self.bank("A")
            for k in range(8):
                self.mm(self.ps[:, b, :], wqi[:, k, c * 128:(c + 1) * 128], self.xnT[:, k, :], start=(k == 0), stop=(k == 7))
            self.copy(self.qiT[:, c, :], self.ps[:, b, :], eng="act" if c % 2 else "dve")
        s_ = self.wslot()
        slot = self.wbuf[s_]
        wc = slot[:, 0:2048].rearrange("p (c n) -> p c n", c=8)
        wk = slot[:, 2048:3072].rearrange("p (c n) -> p c n", c=8)
        wi = slot[:, 3072:3136].rearrange("p (c n) -> p c n", c=8)
        grp = "w%d" % s_
        self.dma("pool", wc, win[:, :, O_CKV:O_CKV + 256], [], [wc], grp)
        self.dma("pool", wk[:, :, 0:64], win[:, :, O_KI:O_KI + 64], [], [wk[:, :, 0:64]], grp)
        self.dma("pool", wk[:, :, 64:128], win[:, :, O_KI:O_KI + 64], [], [wk[:, :, 64:128]], grp)
        self.dma("pool", wi, win[:, :, O_WI:O_WI + 8], [], [wi], grp)
        b = self.bank("A")
        for k in range(8):
            self.mm(self.ps[:, b, :], wk[:, k, :], self.xnT[:, k, :], start=(k == 0), stop=(k == 7))
        kf = self.f32t[self.rot("f32t", 2)]
        self.copy(kf[:], self.ps[:, b, :])
        sq = self.sq[self.rot("sq", 2)]
        self.act(sq[:], kf[:], AF.Square)
        b = self.bank("A")
        self.mm(self.ps[:, b, :], self.ones_b[0:64, :], sq[0:64, :])
        self.act(self.rstd[:], self.ps[:, b, :], AF.Sqrt, bias=self.eps_t[:, 0:1], scale=1.0 / 64)
        self.op("dve", lambda e: e.reciprocal(out=self.rstd[:], in_=self.rstd[:]), [self.rstd[:]], [self.rstd[:]])
        self.stt(self.kin[:, 0, :], kf[:], self.g_ki[:, l:l + 1], self.rstd[:], ALU.mult, ALU.mult)
        self.dma("sp", self.ic[l, :, t0:t0 + TT], self.kin[:, 0, :], [self.kin[:, 0, :]], [("ic", l, i)], "cw")
        b = self.bank("A")
        for k in range(8):
            self.mm(self.ps[0:8, b, :], wi[:, k, :], self.xnT[:, k, :], start=(k == 0), stop=(k == 7))
        self.copy(self.wiT[:], self.ps[0:8, b, :])
        b = self.bank("A")
        for jj in range(4):
            self.tr(self.ps[:, b, jj * 8:(jj + 1) * 8], self.wiT[:, jj * 128:(jj + 1) * 128], self.ident_f[0:8, 0:8])
        self.copy(self.w_tok[:], self.ps[:, b, 0:32].rearrange("p (j h) -> p j h", h=8))
        nk_all = (i + 1) * TT
        self.dma("sp", self.kidx_sb[:, 0:nk_all], self.ic[l, :, 0:nk_all], [("ic", l, t) for t in range(i + 1)], [self.kidx_sb[:, 0:nk_all]], "kidx")
        kcv = self.kc[l].rearrange("(c p) s -> p c s", p=128)

        def gen_proj():
            (wq,) = self.wload([win[:, :, O_Q:O_Q + 512]])
            for c in range(4):
                b = self.bank("A")
                for k in range(8):
                    self.mm(self.ps[:, b, :], wq[:, k, c * 128:(c + 1) * 128], self.xnT[:, k, :], start=(k == 0), stop=(k == 7))
                self.copy(self.qT[:, c, :], self.ps[:, b, :], eng="act" if c % 2 else "dve")
                yield
            for c in range(2):
                b = self.bank("A")
                for k in range(8):
                    self.mm(self.ps[:, b, :], wc[:, k, c * 128:(c + 1) * 128], self.xnT[:, k, :], start=(k == 0), stop=(k == 7))
                self.copy(self.ckv_f[:, c, :], self.ps[:, b, :], eng="act")
            yield
            self.rmsnorm_T(self.ckv_f[:, :, :], 2, lambda c: self.g_kv[:, l, c:c + 1], self.ckvn[:, :, :], 256)
            yield
            wuk_v = W["w_uk"][l].rearrange("(c p) n -> p c n", p=128)
            wuv_v = W["w_uv"][l].rearrange("(c p) n -> p c n", p=128)
            wuk, wuv = self.wload([wuk_v, wuv_v])
            for c in range(4):
                b = self.bank("A")
                for k in range(2):
                    self.mm(self.ps[:, b, :], wuk[:, k, c * 128:(c + 1) * 128], self.ckvn[:, k, :], start=(k == 0), stop=(k == 1))
                self.copy(self.kt_out[:, c, :], self.ps[:, b, :], eng="act" if c % 2 else "dve")
                yield
            self.dma("sp", self.kc[l].rearrange("(c p) s -> p c s", p=128)[:, :, t0:t0 + TT], self.kt_out[:], [self.kt_out[:]], [("kc", l, i)], "cw")
            v4 = self.v_out[:].rearrange("p c (h e) -> p c h e", e=72)
            for sbk in range(4):
                self.memset(v4[:, sbk, :, 64:65], 1.0)
                b = self.bank("A")
                for k in range(2):
                    self.mm(self.ps[:, b, :], self.ckvn[:, k, sbk * 128:(sbk + 1) * 128], wuv[:, k, :], start=(k == 0), stop=(k == 1))
                self.copy(v4[:, sbk, :, 0:64], self.ps[:, b, :].rearrange("p (h e) -> p h e", e=64), eng="act" if sbk % 2 else "dve")
                yield
            self.dma("sp", self.vc[l, t0:t0 + TT, :].rearrange("(c p) e -> p c e", p=128), self.v_out[:], [self.v_out[:]], [("vc", l, i)], "cw")

        def pre(jj, pool="A"):
            jg = 4 * i + jj
            n = 128 * (jg + 1)
            tq = slice(jj * 128, (jj + 1) * 128)
            acc = self.idx_acc
            for h in range(8):
                if h:
                    yield
                for c0 in range(0, n, 512):
                    N = min(512, n - c0)
                    b = self.bank(pool)
                    po = (h % 2) * 64
                    self.mm(self.ps[:, b, 0:N], self.qiT[po:po + 64, h // 2, tq], self.kidx_sb[po:po + 64, c0:c0 + N])
                    tmp = self.rtmp[self.rot("rtmp", 2)]
                    self.act(tmp[:, 0:N], self.ps[:, b, 0:N], AF.Relu)
                    if h == 0:
                        self.ts(acc[:, c0:c0 + N], tmp[:, 0:N], self.w_tok[:, jj, 0:1], ALU.mult)
                    else:
                        self.stt(acc[:, c0:c0 + N], tmp[:, 0:N], self.w_tok[:, jj, h:h + 1], acc[:, c0:c0 + N], ALU.mult, ALU.add)
            self.tt(acc[:, n - 128:n], acc[:, n - 128:n], self.neg_tri[:], ALU.add)

        def gen_bis(jj):
            jg = 4 * i + jj
            n = 128 * (jg + 1)
            acc = self.idx_acc
            if jg >= 2:
                yield from self.bisect(acc, n)
            else:
                self.memset(self.thr[:], -1.0e29)
            self.ts(self.masks[jj % 2][:, 0:n], acc[:, 0:n], self.thr[:, 0:1], ALU.is_lt, s2=-30000.0, op1=ALU.mult)
            yield

        def gen_p2(jj):
            jg = 4 * i + jj
            n = 128 * (jg + 1)
            tq = slice(jj * 128, (jj + 1) * 128)
            mask_ = self.masks[jj % 2]
            nsb = jg + 1
            slot_of = {}

            for h in range(8):
                po = (h % 2) * 64
                self.copy(self.qz[po:po + 64, h, :], self.qT[po:po + 64, h // 2, tq], eng="act" if h % 2 else "dve")

            def emit_TS(k):
                c, sbk = k // 4, k % 4
                if sbk == 0:
                    c0 = c * 512
                    nb = min(4, nsb - c * 4)
                    sl_ = self.rot("kvst", 2)
                    slot_of[c] = sl_
                    kst, vst = self.kst[sl_], self.vst[sl_]
                    self.dma("sp", kst[:, :, 0:nb * 128], kcv[:, :, c0:c0 + nb * 128], [("kc", l, c)], [kst[:, :, 0:nb * 128]], "kv%d" % sl_)
                    self.dma("sp", vst[:, 0:nb, :], self.vc[l, c0:c0 + nb * 128, :].rearrange("(c p) e -> p c e", p=128), [("vc", l, c)],
                             [vst[:, 0:nb, :]], "kv%d" % sl_)
                kst = self.kst[slot_of[c]]
                sb0 = 2 + 2 * (k % 2)
                for h in range(8):
                    reg = self.ps[:, sb0 + h // 4, (h % 4) * 128:(h % 4 + 1) * 128]
                    self.mm(reg, kst[:, h // 2, sbk * 128:(sbk + 1) * 128], self.qz[:, h, :], start=True, stop=False, skip_group_check=True)
                    self.mm(reg, mask_[:, k * 128:(k + 1) * 128], self.ident_b[:], start=False, stop=True, skip_group_check=True)

            def emit_rest(k):
                c, sbk = k // 4, k % 4
                vst = self.vst[slot_of[c]]
                sb0 = 2 + 2 * (k % 2)
                pT = self.pTt[k % 2]
                self.act(pT[:], self.ps[:, sb0:sb0 + 2, :].rearrange("p b (h t) -> p (b h) t", t=128), AF.Exp, scale=0.125)
                for h in range(8):
                    self.mm(self.ps[:, 6 + h // 4, (h % 4) * 72:(h % 4) * 72 + 65], pT[:, h, :], vst[:, sbk, h * 72:h * 72 + 65],
                            start=(k == 0 and h % 4 == 0), stop=(k == nsb - 1), skip_group_check=True)

            emit_TS(0)
            for k in range(nsb):
                if k + 1 < nsb:
                    emit_TS(k + 1)
                emit_rest(k)
                yield
            ov = self.ps[:, 6:8, 0:288].rearrange("p b (h e) -> p b h e", e=72)
            rc = self.rcp[:].rearrange("p (b h) -> p b h", b=2)
            for bb_ in range(2):
                den = ov[:, bb_, :, 64:65].rearrange("p h e -> p (h e)")
                self.op("dve", lambda e, o_=rc[:, bb_, :], d_=den: e.reciprocal(out=o_, in_=d_), [den], [rc[:, bb_, :]])
            o3 = self.o_sb[:].rearrange("p (h e) -> p h e", e=64)
            for h in range(8):
                self.ts(o3[:, h, :], ov[:, h // 4, h % 4, 0:64], self.rcp[:, h:h + 1], ALU.mult)
            b = self.bank("A")
            for c in range(4):
                self.tr(self.ps_b[:, b, c * 128:(c + 1) * 128], self.o_sb[:, c * 128:(c + 1) * 128], self.ident_b[:])
            self.copy(self.y_attnT[:, :, tq], self.ps_b[:, b, 0:512].rearrange("p (c n) -> p c n", c=4), eng="act")

            yield

        def drain(g):
            for _ in g:
                pass

        def interleave(ga, na, gb, nb_):
            da = db = False
            ia = ib = 0
            while not (da and db):
                if not da and (db or ia * max(nb_, 1) <= ib * max(na, 1)):
                    try:
                        next(ga)
                        ia += 1
                    except StopIteration:
                        da = True
                elif not db:
                    try:
                        next(gb)
                        ib += 1
                    except StopIteration:
                        db = True

        self.memset(self.qz[:], 0.0)

        def g_next(jj, pool):
            yield from pre(jj, pool)
            yield
            yield from gen_bis(jj)

        interleave(g_next(0, "A2"), NBIS + 9, gen_proj(), 15)
        for jj in range(4):
            if jj + 1 < 4:
                interleave(g_next(jj + 1, "A2"), NBIS + 9, gen_p2(jj), 4 * i + jj + 2)
            else:
                drain(gen_p2(jj))

    def bisect(self, acc, n):
        lo, hi, zA, c_, t1, cntB, d_, z_ = (self.bis[:, k, 0:1] for k in range(8))
        junk = self.junk
        split = n >= 1536
        nA = max(256, (n * 52 // 100) // 256 * 256) if split else n
        nB = n - nA
        theta = 255.5 - nB / 2.0
        self.op("dve", lambda e: e.tensor_reduce(out=lo, in_=acc[:, 0:n - 128], axis=AX.X, op=ALU.min), [acc[:, 0:n - 128]], [lo])
        self.op("dve", lambda e: e.tensor_reduce(out=hi, in_=acc[:, 0:n], axis=AX.X, op=ALU.max), [acc[:, 0:n]], [hi])
        self.tt(t1, hi, lo, ALU.subtract)
        self.ts(self.hks[:], self.pw2[:], t1, ALU.mult, s2=0.5, op1=ALU.mult)
        self.tt(c_, lo, self.hks[:, 0:1], ALU.add)
        for k in range(NBIS):
            hk = self.hks[:, k:k + 1]
            self.ts(junk[:, 0:nA], acc[:, 0:nA], c_, ALU.is_ge, s2=-theta, op1=ALU.add, accum_out=zA)
            if split:
                self.act(junk[:, nA:n], acc[:, nA:n], AF.Sign, bias=c_, scale=-1.0, accum_out=cntB)
            self.stt(d_, hk, -0.5, c_, ALU.mult, ALU.add)
            if split:
                self.stt(z_, cntB, -0.5, zA, ALU.mult, ALU.add)
                zz = z_
            else:
                zz = zA
            self.ts(t1, zz, 0.0, ALU.is_ge, s2=hk, op1=ALU.mult)
            self.tt(c_, t1, d_, ALU.add)
            yield
        self.tt(self.thr[:], c_, self.hks[:, NBIS:NBIS + 1], ALU.subtract)

    def ple(self, i, l):
        W = self.W
        g = self.g_ple
        self.rmsnorm_T(self.hT[:, :, :], 8, lambda c: g[:, l, c:c + 1], self.xnT[:, :, :], D)
        for j in range(4):
            xb = self.xin[self.rot("xin", 2)]
            r0 = i * TT + j * 128
            self.dma("sp", xb[:, 0:256], self.p[l, r0:r0 + 128, :], [], [xb[:, 0:256]], "xin")
            b = self.bank("A")
            for c in range(2):
                self.tr(self.ps[:, b, c * 128:(c + 1) * 128], xb[:, c * 128:(c + 1) * 128], self.ident_f[:])
            self.copy(self.pT[:, :, j * 128:(j + 1) * 128], self.ps[:, b, 0:256].rearrange("p (c n) -> p c n", c=2), eng="act")
        wg_v = W["ple_w_gate"][l].rearrange("(c p) n -> p c n", p=128)
        wp_v = W["ple_w_proj"][l].rearrange("(c p) n -> p c n", p=128)
        for half in range(2):
            (wg,) = self.wload([wg_v[:, :, half * 512:(half + 1) * 512]])
            (wp,) = self.wload([wp_v[:, :, half * 512:(half + 1) * 512]])
            for dc in range(4):
                bg = self.bank("A")
                bp = self.bank("A")
                for k in range(8):
                    self.mm(self.ps[:, bg, :], wg[:, k, dc * 128:(dc + 1) * 128], self.xnT[:, k, :], start=(k == 0), stop=(k == 7))
                for k in range(2):
                    self.mm(self.ps[:, bp, :], wp[:, k, dc * 128:(dc + 1) * 128], self.pT[:, k, :], start=(k == 0), stop=(k == 1))
                self.act(self.pleg[:], self.ps[:, bg, :], AF.Sigmoid)
                tmp = self.f32t[self.rot("f32t", 2)]
                self.tt(tmp[:], self.pleg[:], self.ps[:, bp, :], ALU.mult)
                hc = self.hT[:, half * 4 + dc, :]
                self.tt(hc, hc, tmp[:], ALU.add)

    def final(self, i):
        b = self.bank("A")
        acc = self.ps[:, b, :]
        for c in range(8):
            sq = self.sq[self.rot("sq", 2)]
            self.act(sq[:], self.hT[:, c, :], AF.Square)
            self.mm(acc, self.ones_b[:], sq[:], start=(c == 0), stop=(c == 7))
        self.act(self.rstd[:], acc, AF.Sqrt, bias=self.eps_t[:, 0:1], scale=1.0 / D)
        self.op("dve", lambda e: e.reciprocal(out=self.rstd[:], in_=self.rstd[:]), [self.rstd[:]], [self.rstd[:]])
        for c in range(8):
            self.stt(self.hT[:, c, :], self.hT[:, c, :], self.g_fin[:, c:c + 1], self.rstd[:], ALU.mult, ALU.mult)
        for j in range(4):
            xb = self.xin[self.rot("xin", 2)]
            for half in range(2):
                b = self.bank("A")
                for c in range(4):
                    cc = half * 4 + c
                    self.tr(self.ps[:, b, c * 128:(c + 1) * 128], self.hT[:, cc, j * 128:(j + 1) * 128], self.ident_f[:])
                self.copy(xb[:, half * 512:(half + 1) * 512], self.ps[:, b, :], eng="act" if half else "dve")
            r0 = i * TT + j * 128
            self.dma("sp", self.out[r0:r0 + 128, :], xb[:], [xb[:]], [("out", r0)], "out")


_CACHE = {}


def get_nc(NT, NL, mix=7):
    key = (NT, NL, mix)
    if key not in _CACHE:
        _CACHE[key] = Builder(NT, NL, mix).build()
    return _CACHE[key]


W_NAMES = ["ffn1_norm", "ffn1_w_gate", "ffn1_w_up", "ffn1_w_down", "mix_norm", "w_in", "kv_norm", "idx_k_norm",
           "w_uk", "w_uv", "pool_w", "pool_scale", "conv_w", "conv_b", "dt_bias", "a_log", "d_skip", "ssd_norm",
           "w_br_attn", "w_br_pool", "w_br_ssd", "w_out", "ffn2_norm", "ffn2_w_gate", "ffn2_w_up", "ffn2_w_down",
           "ple_norm", "ple_w_gate", "ple_w_proj", "final_norm"]


def run(inputs, NT, NL, ncores=8, mix=7):
    nc = get_nc(NT, NL, mix)
    S = NT * TT
    shared = {}
    for n in W_NAMES:
        a = np.ascontiguousarray(np.asarray(inputs[n], dtype=np.float32))
        if n in ("w_uk", "w_uv"):
            a = a.reshape(L_FULL, 256, 512)
        shared[n] = a
    x = np.asarray(inputs["x"], dtype=np.float32)
    p = np.asarray(inputs["p"], dtype=np.float32)
    in_maps = []
    for b in range(ncores):
        m = dict(shared)
        m["x"] = np.ascontiguousarray(x[b, :S])
        m["p"] = np.ascontiguousarray(p[:, b, :S])
        in_maps.append(m)
    res = run_bass_kernel_spmd(nc, in_maps, core_ids=list(range(ncores)))
    return np.stack([np.asarray(r["out"]) for r in res.results], axis=0)


def kernel(**inputs):
    return run(inputs, S_FULL // TT, L_FULL, 8).astype(np.float32)
```

```python
import numpy as np
import concourse.bass as bass
import concourse.mybir as mybir
from concourse.bass_utils import run_bass_kernel_spmd

F32 = mybir.dt.float32
BF16 = mybir.dt.bfloat16
I32 = mybir.dt.int32
I8 = mybir.dt.int8
ALU = mybir.AluOpType
AF = mybir.ActivationFunctionType
AX = mybir.AxisListType

D = 1024
S_FULL = 4096
L_FULL = 4
DFF = 2816
NFF = DFF // 128
TT = 512
EPS = 1e-6
NEG = -1.0e30
O_Q, O_CKV, O_QI, O_WI, O_KI, O_XP, O_Z, O_XBC, O_DT, O_GATE = 0, 512, 768, 1280, 1288, 1352, 1864, 2888, 4424, 4440
IN_COLS = 7512
NBIS = 22

_DT_SIZE = {F32: 4, BF16: 2, I32: 4, I8: 1}


def _dsize(dt):
    for k, v in _DT_SIZE.items():
        if dt == k:
            return v
    return 4


class Op:
    __slots__ = ("eng", "fn", "idx", "is_dma", "grp", "waits")


class Sched:
    G = 256
    ENGS = ("pe", "dve", "act", "pool", "sp")

    def __init__(self):
        self.streams = {e: [] for e in self.ENGS}
        self.ncomp = {e: 0 for e in self.ENGS}
        self.last_w = {}
        self.readers = {}
        self.grp_count = {}
        self.seen = {e: {} for e in self.ENGS}

    def keys_of(self, x):
        if isinstance(x, (str, tuple)):
            return [x]
        sp = str(x.space)
        if "DRAM" in sp:
            raise ValueError("DRAM AP needs manual key")
        pairs = list(x.ap)
        pitch = pairs[0][0]
        esz = _dsize(x.dtype)
        lo = x.offset % pitch if pitch > 0 else x.offset
        ext = 0
        for st, cnt in pairs[1:]:
            ext += abs(st) * (cnt - 1)
        hi = lo + ext + 1
        lo_b, hi_b = lo * esz, hi * esz
        name = x.tensor.name
        return [(name, g) for g in range(lo_b // self.G, (hi_b - 1) // self.G + 1)]

    def add(self, eng, fn, reads=(), writes=(), dma=None):
        op = Op()
        op.eng, op.fn, op.is_dma, op.grp = eng, fn, dma is not None, dma
        rk = [k for r in reads for k in self.keys_of(r)]
        wk = [k for w in writes for k in self.keys_of(w)]
        deps = set()
        for k in rk:
            o = self.last_w.get(k)
            if o is not None:
                deps.add(o)
        for k in wk:
            o = self.last_w.get(k)
            if o is not None:
                deps.add(o)
            for o in self.readers.get(k, ()):
                deps.add(o)
        deps.discard(op)
        waits = {}
        for d in deps:
            if d.is_dma:
                sem = "g_" + d.grp
                val = self.grp_count[d.grp]
            else:
                if d.eng == eng and not op.is_dma and eng == "pe":
                    continue
                sem = "e_" + d.eng
                val = d.idx
            if waits.get(sem, 0) < val:
                waits[sem] = val
        seen = self.seen[eng]
        op.waits = []
        for sem, val in waits.items():
            if seen.get(sem, 0) < val:
                seen[sem] = val
                op.waits.append((sem, val))
        if op.is_dma:
            self.grp_count[dma] = self.grp_count.get(dma, 0) + 16
            op.idx = self.grp_count[dma]
        else:
            self.ncomp[eng] += 1
            op.idx = self.ncomp[eng]
        for k in wk:
            self.last_w[k] = op
            self.readers[k] = []
        for k in rk:
            self.readers.setdefault(k, []).append(op)
        self.streams[eng].append(op)
        return op

    def emit(self, nc, final_waits):
        names = ["e_" + e for e in self.ENGS] + ["g_" + g for g in self.grp_count]
        sems = {}
        import contextlib
        with contextlib.ExitStack() as es:
            for n in names:
                sems[n] = es.enter_context(nc.semaphore(n))
            block = es.enter_context(nc.Block())

            def run(engname):
                def body(eng):
                    for op in self.streams[engname]:
                        for sem, val in op.waits:
                            eng.wait_ge(sems[sem], val)
                        ins = op.fn(eng)
                        if op.is_dma:
                            ins.then_inc(sems["g_" + op.grp], 16)
                        else:
                            ins.then_inc(sems["e_" + engname], 1)
                    if engname == "sp":
                        for g in final_waits:
                            eng.wait_ge(sems["g_" + g], self.grp_count[g])
                return body

            block.tensor(run("pe"))
            block.vector(run("dve"))
            block.scalar(run("act"))
            block.gpsimd(run("pool"))
            block.sync(run("sp"))


class Builder:
    def __init__(self, NT, NL, mix=7):
        self.NT, self.NL, self.mix = NT, NL, mix
        self.S = NT * TT
        self.nc = bass.Bass("TRN2", target_bir_lowering=False)
        self.sc = Sched()
        self.psrr = 0
        self.wrr = 0
        self.rr = {}

    def op(self, eng, fn, reads, writes):
        return self.sc.add(eng, fn, reads, writes)

    def dma(self, q, out, in_, reads, writes, grp, **kw):
        return self.sc.add(q, lambda e: e.dma_start(out=out, in_=in_, **kw), reads, writes, dma=grp)

    def mm(self, out, lhsT, rhs, start=True, stop=True, extra_reads=(), **kw):
        return self.op("pe", lambda e: e.matmul(out, lhsT=lhsT, rhs=rhs, start=start, stop=stop, **kw),
                       [lhsT, rhs] + list(extra_reads), [out])

    def tr(self, out, in_, ident):
        return self.op("pe", lambda e: e.transpose(out, in_, ident), [in_, ident], [out])

    def act(self, out, in_, func, bias=None, scale=None, eng="act", extra_reads=(), accum_out=None):
        kw = {}
        rd = [in_] + list(extra_reads)
        wr_extra = []
        if accum_out is not None:
            kw["accum_out"] = accum_out
            wr_extra.append(accum_out)
        if bias is not None:
            kw["bias"] = bias
            if not isinstance(bias, (int, float)):
                rd.append(bias)
        if scale is not None:
            kw["scale"] = scale
            if not isinstance(scale, (int, float)):
                rd.append(scale)
        return self.op(eng, lambda e: e.activation(out=out, in_=in_, func=func, **kw), rd, [out] + wr_extra)

    def ts(self, out, in0, s1, op0, s2=None, op1=None, eng="dve", accum_out=None):
        rd = [in0]
        for s in (s1, s2):
            if s is not None and not isinstance(s, (int, float)):
                rd.append(s)
        kw = {}
        if op1 is not None:
            kw["op1"] = op1
        wr = [out]
        if accum_out is not None:
            kw["accum_out"] = accum_out
            wr.append(accum_out)
        return self.op(eng, lambda e: e.tensor_scalar(out=out, in0=in0, scalar1=s1, scalar2=s2, op0=op0, **kw), rd, wr)

    def tt(self, out, in0, in1, op, eng="dve"):
        return self.op(eng, lambda e: e.tensor_tensor(out=out, in0=in0, in1=in1, op=op), [in0, in1], [out])

    def stt(self, out, in0, scalar, in1, op0, op1):
        rd = [in0, in1]
        if not isinstance(scalar, (int, float)):
            rd.append(scalar)
        return self.op("dve", lambda e: e.scalar_tensor_tensor(out=out, in0=in0, scalar=scalar, in1=in1, op0=op0, op1=op1), rd, [out])

    def copy(self, out, in_, eng="dve"):
        if eng == "act":
            return self.op("act", lambda e: e.copy(out=out, in_=in_), [in_], [out])
        return self.op(eng, lambda e: e.tensor_copy(out=out, in_=in_), [in_], [out])

    def memset(self, ap, val, eng="dve"):
        return self.op(eng, lambda e: e.memset(ap, val), [], [ap])

    def sb(self, name, shape, dt):
        return self.es.enter_context(self.nc.sbuf_tensor(name, list(shape), dt))

    def bank(self, pool="A"):
        banks = {"A": (0, 1, 2, 3), "B": (4, 5, 6, 7), "A2": (0, 1)}[pool]
        i = self.rr.get(pool, 0)
        self.rr[pool] = i + 1
        return banks[i % len(banks)]

    def rot(self, name, n):
        i = self.rr.get(name, 0)
        self.rr[name] = i + 1
        return i % n

    def wslot(self):
        i = self.wrr
        self.wrr += 1
        return i % self.NW

    def wload(self, parts):
        s = self.wslot()
        slot = self.wbuf[s]
        off = 0
        views = []
        for src in parts:
            shp = list(src.shape)
            n = 1
            for v in shp[1:]:
                n *= v
            dst = slot[0:shp[0], off:off + n]
            if len(shp) == 3:
                dst = dst.rearrange("p (c n) -> p c n", c=shp[1])
            self.dma("pool", dst, src, [], [dst], "w%d" % s)
            views.append(dst)
            off += n
        assert off <= self.WSZ, off
        return views

    def build(self):
        import contextlib
        nc = self.nc
        NT, NL, S = self.NT, self.NL, self.S
        with contextlib.ExitStack() as es:
            self.es = es
            dt_in = lambda name, shape: nc.dram_tensor(name, list(shape), F32, kind="ExternalInput").ap()
            self.x = dt_in("x", [S, D])
            self.p = dt_in("p", [L_FULL, S, 256])
            W = {}
            for name, shape in [
                ("ffn1_norm", [L_FULL, D]), ("ffn1_w_gate", [L_FULL, D, DFF]), ("ffn1_w_up", [L_FULL, D, DFF]),
                ("ffn1_w_down", [L_FULL, DFF, D]), ("mix_norm", [L_FULL, D]), ("w_in", [L_FULL, D, IN_COLS]),
                ("kv_norm", [L_FULL, 256]), ("idx_k_norm", [L_FULL, 64]), ("w_uk", [L_FULL, 256, 512]),
                ("w_uv", [L_FULL, 256, 512]), ("pool_w", [L_FULL, 4, 128, 128]), ("pool_scale", [L_FULL, 512]),
                ("conv_w", [L_FULL, 4, 1536]), ("conv_b", [L_FULL, 1536]), ("dt_bias", [L_FULL, 16]),
                ("a_log", [L_FULL, 16]), ("d_skip", [L_FULL, 16]), ("ssd_norm", [L_FULL, D]),
                ("w_br_attn", [L_FULL, 512, D]), ("w_br_pool", [L_FULL, 512, D]), ("w_br_ssd", [L_FULL, D, D]),
                ("w_out", [L_FULL, D, D]), ("ffn2_norm", [L_FULL, D]), ("ffn2_w_gate", [L_FULL, D, DFF]),
                ("ffn2_w_up", [L_FULL, D, DFF]), ("ffn2_w_down", [L_FULL, DFF, D]), ("ple_norm", [L_FULL, D]),
                ("ple_w_gate", [L_FULL, D, D]), ("ple_w_proj", [L_FULL, 256, D]), ("final_norm", [D]),
            ]:
                W[name] = dt_in(name, shape)
            self.W = W
            self.out = nc.dram_tensor("out", [S, D], F32, kind="ExternalOutput").ap()
            self.kc = nc.dram_tensor("kcache", [L_FULL, 512, S], BF16, kind="Internal").ap()
            self.vc = nc.dram_tensor("vcache", [L_FULL, S, 576], BF16, kind="Internal").ap()
            self.ic = nc.dram_tensor("icache", [L_FULL, 128, S], BF16, kind="Internal").ap()

            self.alloc()
            self.init_consts()
            for i in range(NT):
                self.load_x_tile(i)
                for l in range(NL):
                    self.ffn(i, l, 1)
                    self.mixer(i, l)
                    self.ffn(i, l, 2)
                    self.ple(i, l)
                self.final(i)
            self.sc.emit(nc, ["out"])
        return nc

    def alloc(self):
        sb = self.sb
        nc = self.nc
        self.ps = self.es.enter_context(nc.psum_tensor("ps", [128, 8, 512], F32))
        self.NW, self.WSZ = 5, 4096
        self.wbuf = [sb("wbuf%d" % i, [128, self.WSZ], BF16) for i in range(self.NW)]
        self.hT = sb("hT", [128, 8, TT], F32)
        self.xnT = sb("xnT", [128, 8, TT], BF16)
        AR = 48992 + 672
        self.arena = sb("arena", [128, AR], BF16)
        self.ar_off = 0

        def carve(nel, dt, shape=None, at=None):
            esz = _dsize(dt)
            if at is None:
                at = self.ar_off
            assert at % 4 == 0
            nb = nel * esz
            assert at + nb <= AR * 2, (at, nb)
            v = self.arena[:, at // 2: (at + nb) // 2]
            if dt != BF16:
                v = v.bitcast(dt)
            self.ar_off = at + nb
            return v
        self.carve = carve
        K = 1024
        self.actT = carve(NFF * TT, BF16, at=0).rearrange("p (c n) -> p c n", c=NFF)
        self.m = carve(8 * TT, F32, at=0).rearrange("p (c n) -> p c n", c=8)
        self.xin = [carve(D, F32, at=22 * K + j * 4 * K) for j in range(2)]
        self.qz = carve(8 * 128, BF16, at=20 * K).rearrange("p (h n) -> p h n", h=8)
        BASE = 22 * K
        o = BASE
        self.idx_acc = carve(4096, F32, at=o)
        o += 16 * K
        self.masks = [carve(4096, BF16, at=o + j * 8 * K) for j in range(2)]; o += 16 * K
        self.junk = carve(4096, I8, at=o); o += 4 * K
        self.kidx_sb = carve(4096, BF16, at=o); o += 8 * K
        self.ckv_f = carve(2 * TT, F32, at=o).rearrange("p (c n) -> p c n", c=2)
        self.kt_out = carve(4 * TT, BF16, at=o + 4 * K).rearrange("p (c n) -> p c n", c=4)
        self.v_out = carve(4 * 576, BF16, at=o + 8 * K).rearrange("p (c n) -> p c n", c=4)
        self.kst = [carve(4 * 512, BF16, at=o + j * 4 * K).rearrange("p (c n) -> p c n", c=4) for j in range(2)]; o += 8 * K
        self.vst = [carve(4 * 576, BF16, at=o + j * 4608).rearrange("p (c n) -> p c n", c=4) for j in range(2)]; o += 2 * 4608
        self.pTt = [carve(8 * 128, BF16, at=o + j * 2 * K).rearrange("p (c n) -> p c n", c=8) for j in range(2)]; o += 4 * K
        self.rtmp = [carve(512, F32, at=o + j * 2 * K) for j in range(2)]; o += 4 * K
        self.maskT = [carve(128, BF16, at=o + j * 256) for j in range(2)]; o += 512
        self.o_sb = carve(512, BF16, at=o); o += K
        self.y_attnT = carve(4 * TT, BF16, at=o).rearrange("p (c n) -> p c n", c=4); o += 4 * K
        self.att_end = o
        o = BASE
        self.xpT = carve(4 * 528, F32, at=o).rearrange("p (c n) -> p c n", c=4); o += 4 * 528 * 4
        self.stmp = [carve(528, F32, at=o + j * 2112) for j in range(2)]; o += 2 * 2112
        self.pooledT = carve(4 * TT, BF16, at=o).rearrange("p (c n) -> p c n", c=4); o += 4 * K
        self.y_poolT = carve(4 * TT, BF16, at=o).rearrange("p (c n) -> p c n", c=4); o += 4 * K
        self.tmp16 = carve(16, F32, at=o); o += 64
        o = BASE
        self.zs = carve(8 * TT, BF16, at=o).rearrange("p (c n) -> p c n", c=8); o += 8 * K
        self.xsT = carve(8 * TT, F32, at=o).rearrange("p (c n) -> p c n", c=8); o += 16 * K
        self.BT = carve(2 * TT, BF16, at=o).rearrange("p (c n) -> p c n", c=2); o += 2 * K
        self.CT = carve(2 * TT, BF16, at=o).rearrange("p (c n) -> p c n", c=2); o += 2 * K
        self.dtT = carve(TT, F32, at=o); o += 2 * K
        self.daT = carve(TT, F32, at=o); o += 2 * K
        self.acsT = carve(TT, F32, at=o); o += 2 * K
        self.decT = carve(TT, F32, at=o); o += 2 * K
        self.eaT = carve(TT, F32, at=o); o += 2 * K
        self.cacc = [carve(TT, F32, at=o + j * 2 * K) for j in range(2)]; o += 4 * K
        self.Xb = carve(1024, BF16, at=o).rearrange("p (h e) -> p h e", h=16); o += 2 * K
        self.Xd = carve(1024, BF16, at=o).rearrange("p (h e) -> p h e", h=16); o += 2 * K
        self.tok = carve(64, F32, at=o); o += 256
        self.etot = carve(16, F32, at=o); o += 64
        self.dg = carve(16, F32, at=o); o += 64
        self.yt1 = carve(1024, F32, at=o).rearrange("p (h e) -> p h e", h=16); o += 4 * K
        self.ytok = carve(1024, F32, at=o); o += 4 * K
        self.Sb = carve(1024, BF16, at=o); o += 2 * K
        self.Btok = carve(256, BF16, at=o).rearrange("p (c n) -> p c n", c=2); o += 512
        self.CBm = carve(256, F32, at=o).rearrange("p (c n) -> p c n", c=2); o += K
        self.A_all = carve(2048, F32, at=o).rearrange("p (h n) -> p h n", h=16); o += 8 * K
        self.Mh_all = carve(2048, BF16, at=o).rearrange("p (h n) -> p h n", h=16); o += 4 * K
        self.stmp2 = carve(512, F32, at=o); o += 2 * K
        self.ssd_end = o
        assert max(self.att_end, self.ssd_end) <= AR * 2, (self.att_end, self.ssd_end)
        self.mergedT = carve(8 * TT, BF16, at=BASE).rearrange("p (c n) -> p c n", c=8)
        self.qT = carve(4 * TT, BF16, at=16 * K).rearrange("p (c n) -> p c n", c=4)
        self.qiT = sb("qiT", [128, 4, TT], BF16)
        self.ckvn = sb("ckvn", [128, 2, TT], BF16)
        self.kin = sb("kin", [128, 1, TT], BF16)
        self.wiT = sb("wiT", [8, TT], F32)
        self.w_tok = sb("w_tok", [128, 4, 8], F32)
        self.S_f = sb("S_f", [128, L_FULL, 1024], F32)
        self.chalo = sb("chalo", [128, L_FULL, 12, 3], F32)
        self.phalo = sb("phalo", [128, L_FULL, 4, 16], F32)
        self.bis = sb("bis", [128, 8, 64], F32)
        self.hks = sb("hks", [128, NBIS + 1], F32)
        self.pw2 = sb("pw2", [128, NBIS + 1], F32)
        self.thr = sb("thr", [128, 1], F32)
        self.rcp = sb("rcp", [128, 8], F32)
        self.neg_tri = sb("neg_tri", [128, 128], F32)
        self.tri_ml = sb("tri_ml", [128, 128], F32)
        self.mgt = sb("mgt", [128, 128], F32)
        self.ones_f = sb("ones_f", [128, 128], F32)
        self.one_t = sb("one_t", [128, 1], F32)
        self.invcnt = sb("invcnt", [128, 4, 16], F32)
        self.g_kv = sb("g_kv", [128, L_FULL, 2], F32)
        self.g_ki = sb("g_ki", [128, L_FULL], F32)
        self.pscale = sb("pscale", [128, L_FULL, 4], F32)
        self.convw = sb("convw", [128, L_FULL, 4, 12], F32)
        self.convb = sb("convb", [128, L_FULL, 12], F32)
        self.dtb = sb("dtb", [16, L_FULL], F32)
        self.negA = sb("negA", [16, L_FULL], F32)
        self.dexp = sb("dexp", [128, L_FULL, 8], F32)
        self.g_ssd = sb("g_ssd", [128, L_FULL, 8], F32)
        self.ident_f = sb("ident_f", [128, 128], F32)
        self.ident_b = sb("ident_b", [128, 128], BF16)
        self.ones_b = sb("ones_b", [128, 128], BF16)
        self.iot = sb("iot", [128, 128], I32)
        self.sq = [sb("sq%d" % i, [128, TT], BF16) for i in range(2)]
        self.rstd = sb("rstd", [128, TT], F32)
        self.f32t = [sb("f32t%d" % i, [128, TT], F32) for i in range(2)]
        self.ps_b = self.ps[:].bitcast(BF16)
        self.pT = sb("pTin", [128, 2, TT], BF16)
        self.pleg = self.rstd
        self.g_ffn1 = sb("g_ffn1", [128, L_FULL, 8], F32)
        self.g_mix = sb("g_mix", [128, L_FULL, 8], F32)
        self.g_ffn2 = sb("g_ffn2", [128, L_FULL, 8], F32)
        self.g_ple = sb("g_ple", [128, L_FULL, 8], F32)
        self.g_fin = sb("g_fin", [128, 8], F32)
        self.eps_t = sb("eps_t", [128, 1], F32)

    def init_consts(self):
        nc = self.nc
        self.op("pool", lambda e: e.iota(self.iot[:], pattern=[[1, 128]], base=0, channel_multiplier=-1), [], [self.iot[:]])
        self.ts(self.ident_f[:], self.iot[:], 0.0, ALU.is_equal)
        self.ts(self.ident_b[:], self.iot[:], 0.0, ALU.is_equal)
        self.memset(self.ones_b[:], 1.0)
        self.memset(self.eps_t[:], EPS)
        self.memset(self.ones_f[:], 1.0)
        self.memset(self.one_t[:], 1.0)
        self.ts(self.neg_tri[:], self.iot[:], 0.0, ALU.is_gt, s2=NEG, op1=ALU.mult)
        self.ts(self.tri_ml[:], self.iot[:], 0.0, ALU.is_ge)
        self.ts(self.mgt[:], self.iot[:], 0.0, ALU.is_lt)
        for k in range(NBIS + 1):
            self.memset(self.pw2[:, k:k + 1], 2.0 ** (-k))
        for g in range(4):
            wg_ = 2 ** (g + 1)
            self.memset(self.invcnt[:, g, :], 1.0 / wg_)
            for t in range(wg_ - 1):
                self.memset(self.invcnt[:, g, t:t + 1], 1.0 / (t + 1))
        self.memset(self.S_f[:], 0.0)
        self.memset(self.chalo[:], 0.0)
        self.memset(self.phalo[:], 0.0)
        Wd_ = self.W
        sm = lambda t, src: self.dma("sp", t, src, [], [t], "init", allow_slow_non_contiguous=True)
        sm(self.g_kv[:], Wd_["kv_norm"].rearrange("l (c p) -> p l c", p=128))
        sm(self.g_ki[0:64, :], Wd_["idx_k_norm"].rearrange("l p -> p l"))
        sm(self.g_ki[64:128, :], Wd_["idx_k_norm"].rearrange("l p -> p l"))
        sm(self.pscale[:], Wd_["pool_scale"].rearrange("l (c p) -> p l c", p=128))
        for l_ in range(L_FULL):
            sm(self.convw[:, l_], Wd_["conv_w"][l_].rearrange("k (c p) -> p k c", p=128))
        sm(self.convb[:], Wd_["conv_b"].rearrange("l (c p) -> p l c", p=128))
        sm(self.dtb[:], Wd_["dt_bias"].rearrange("l h -> h l"))
        sm(self.negA[:], Wd_["a_log"].rearrange("l h -> h l"))
        for half in range(2):
            sm(self.dexp[half * 64:(half + 1) * 64], Wd_["d_skip"][:, half::2].partition_broadcast(64))
        sm(self.g_ssd[:], Wd_["ssd_norm"].rearrange("l (c p) -> p l c", p=128))
        W = self.W
        for t, name in [(self.g_ffn1, "ffn1_norm"), (self.g_mix, "mix_norm"), (self.g_ffn2, "ffn2_norm"), (self.g_ple, "ple_norm")]:
            src = W[name].rearrange("l (c p) -> p l c", p=128)
            self.dma("sp", t[:], src, [], [t[:]], "init", allow_slow_non_contiguous=True)
        self.dma("sp", self.g_fin[:], W["final_norm"].rearrange("(c p) -> p c", p=128), [], [self.g_fin[:]], "init",
                 allow_slow_non_contiguous=True)
        self.act(self.negA[:], self.negA[:], AF.Exp)
        self.ts(self.negA[:], self.negA[:], -1.0, ALU.mult)

    def load_x_tile(self, i):
        for j in range(4):
            xi_ = self.rot("xin", 2)
            xb = self.xin[xi_]
            r0 = i * TT + j * 128
            self.dma("sp", xb[:], self.x[r0:r0 + 128, :], [], [xb[:]], "xin%d" % xi_)
            for half in range(2):
                b = self.bank("A")
                for c in range(4):
                    cc = half * 4 + c
                    self.tr(self.ps[:, b, c * 128:(c + 1) * 128], xb[:, cc * 128:(cc + 1) * 128], self.ident_f[:])
                dst = self.hT[:, half * 4:half * 4 + 4, j * 128:(j + 1) * 128]
                src = self.ps[:, b, :].rearrange("p (c n) -> p c n", c=4)
                self.copy(dst, src, eng="act" if half else "dve")

    def rmsnorm_T(self, src, nch, gcol, dst, dfeat, Kp=128, N=TT):
        b = self.bank("A")
        acc = self.ps[:, b, 0:N]
        for c in range(nch):
            sq = self.sq[self.rot("sq", 2)]
            self.act(sq[0:Kp, 0:N], src[:, c, :], AF.Square)
            self.mm(acc, self.ones_b[0:Kp, :], sq[0:Kp, 0:N], start=(c == 0), stop=(c == nch - 1))
        self.act(self.rstd[:, 0:N], acc, AF.Sqrt, bias=self.eps_t[:, 0:1], scale=1.0 / dfeat)
        self.op("dve", lambda e: e.reciprocal(out=self.rstd[:, 0:N], in_=self.rstd[:, 0:N]), [self.rstd[:, 0:N]], [self.rstd[:, 0:N]])
        for c in range(nch):
            self.stt(dst[:, c, :], src[:, c, :], gcol(c), self.rstd[0:Kp, 0:N], ALU.mult, ALU.mult)

    def ffn(self, i, l, which):
        W = self.W
        pre = "ffn%d_" % which
        g = self.g_ffn1 if which == 1 else self.g_ffn2
        self.rmsnorm_T(self.hT[:, :, :], 8, lambda c: g[:, l, c:c + 1], self.xnT[:, :, :], D)
        wg_v = W[pre + "w_gate"][l].rearrange("(c p) n -> p c n", p=128)
        wu_v = W[pre + "w_up"][l].rearrange("(c p) n -> p c n", p=128)
        wd_v = W[pre + "w_down"][l].rearrange("(c p) n -> p c n", p=128)
        for c0 in range(0, DFF, 512):
            nco = min(512, DFF - c0)
            (wg,) = self.wload([wg_v[:, :, c0:c0 + nco]])
            (wu,) = self.wload([wu_v[:, :, c0:c0 + nco]])
            for j in range(nco // 128):
                f = c0 // 128 + j
                bg = self.bank("A")
                bu = self.bank("A")
                for k in range(8):
                    self.mm(self.ps[:, bg, :], wg[:, k, j * 128:(j + 1) * 128], self.xnT[:, k, :], start=(k == 0), stop=(k == 7))
                for k in range(8):
                    self.mm(self.ps[:, bu, :], wu[:, k, j * 128:(j + 1) * 128], self.xnT[:, k, :], start=(k == 0), stop=(k == 7))
                sg = self.f32t[self.rot("f32t", 2)]
                self.act(sg[:], self.ps[:, bg, :], AF.Silu)
                self.tt(self.actT[:, f, :], sg[:], self.ps[:, bu, :], ALU.mult)
        for half in range(2):
            for f0 in range(0, NFF, 4):
                nf = min(4, NFF - f0)
                (wd,) = self.wload([wd_v[:, f0:f0 + nf, half * 512:(half + 1) * 512]])
                for r in range(nf):
                    f = f0 + r
                    for dc in range(4):
                        self.mm(self.ps[:, 4 + dc, :], wd[:, r, dc * 128:(dc + 1) * 128], self.actT[:, f, :],
                                start=(f == 0), stop=(f == NFF - 1))
            for dc in range(4):
                hc = self.hT[:, half * 4 + dc, :]
                self.stt(hc, self.ps[:, 4 + dc, :], 0.5, hc, ALU.mult, ALU.add)

    def mixer(self, i, l):
        if not self.mix:
            return
        W = self.W
        g = self.g_mix
        self.rmsnorm_T(self.hT[:, :, :], 8, lambda c: g[:, l, c:c + 1], self.xnT[:, :, :], D)
        self.win = W["w_in"][l].rearrange("(c p) n -> p c n", p=128)
        first = True
        if self.mix & 2:
            self.pool_branch(i, l)
            self.lift(l, 1, self.y_poolT, 4, "w_br_pool", first)
            first = False
        if self.mix & 4:
            self.ssd_branch(i, l)
            self.lift(l, 2, self.zs, 8, "w_br_ssd", first)
            first = False
        if self.mix & 1:
            self.att_branch(i, l)
            self.lift(l, 0, self.y_attnT, 4, "w_br_attn", first)
            first = False
        for c in range(8):
            self.copy(self.mergedT[:, c, :], self.m[:, c, :], eng="act" if c % 2 else "dve")
        wo_v = W["w_out"][l].rearrange("(c p) n -> p c n", p=128)
        for half in range(2):
            (wo,) = self.wload([wo_v[:, :, half * 512:(half + 1) * 512]])
            for dc in range(4):
                b = self.bank("A")
                for k in range(8):
                    self.mm(self.ps[:, b, :], wo[:, k, dc * 128:(dc + 1) * 128], self.mergedT[:, k, :], start=(k == 0), stop=(k == 7))
                hc = self.hT[:, half * 4 + dc, :]
                self.tt(hc, hc, self.ps[:, b, :], ALU.add)

    def lift(self, l, b, yT, nk, wname, first):
        wbr = self.W[wname][l].rearrange("(c p) n -> p c n", p=128)
        for half in range(2):
            (wb,) = self.wload([wbr[:, :, half * 512:(half + 1) * 512]])
            g0 = O_GATE + b * 1024 + half * 512
            (wgt,) = self.wload([self.win[:, :, g0:g0 + 512]])
            for dc in range(4):
                bb = self.bank("A")
                for k in range(nk):
                    self.mm(self.ps[:, bb, :], wb[:, k, dc * 128:(dc + 1) * 128], yT[:, k, :], start=(k == 0), stop=(k == nk - 1))
                bg = self.bank("A")
                for k in range(8):
                    self.mm(self.ps[:, bg, :], wgt[:, k, dc * 128:(dc + 1) * 128], self.xnT[:, k, :], start=(k == 0), stop=(k == 7))
                self.act(self.pleg[:], self.ps[:, bg, :], AF.Sigmoid)
                mc = self.m[:, half * 4 + dc, :]
                if first:
                    self.tt(mc, self.pleg[:], self.ps[:, bb, :], ALU.mult)
                else:
                    tmp = self.f32t[self.rot("f32t", 2)]
                    self.tt(tmp[:], self.pleg[:], self.ps[:, bb, :], ALU.mult)
                    self.tt(mc, mc, tmp[:], ALU.add)

    def pool_branch(self, i, l):
        W = self.W
        (wxp,) = self.wload([self.win[:, :, O_XP:O_XP + 512]])
        (wpool,) = self.wload([W["pool_w"][l].rearrange("g c d -> c g d")])
        for g in range(4):
            b = self.bank("A")
            for k in range(8):
                self.mm(self.ps[:, b, :], wxp[:, k, g * 128:(g + 1) * 128], self.xnT[:, k, :], start=(k == 0), stop=(k == 7))
            xp = self.xpT[:, g, :]
            self.copy(xp[:, 16:528], self.ps[:, b, :], eng="act")
            self.copy(xp[:, 0:16], self.phalo[:, l, g, :])
            self.copy(self.phalo[:, l, g, :], xp[:, 512:528])
            cur = xp
            a = 0
            for step in range(g + 1):
                sh = 2 ** step
                a += sh
                nxt = self.stmp[self.rot("stmp", 2)]
                self.tt(nxt[:, a:528], cur[:, a:528], cur[:, a - sh:528 - sh], ALU.add)
                cur = nxt
            wg_ = 2 ** (g + 1)
            self.stt(self.pooledT[:, g, :], cur[:, 16:528], 1.0 / wg_, xp[:, 16:528], ALU.mult, ALU.subtract)
            if i == 0:
                self.tt(self.tmp16[:], cur[:, 16:32], self.invcnt[:, g, :], ALU.mult)
                self.tt(self.pooledT[:, g, 0:16], self.tmp16[:], xp[:, 16:32], ALU.subtract)
            b2 = self.bank("A")
            self.mm(self.ps[:, b2, :], wpool[:, g, :], self.pooledT[:, g, :])
            self.ts(self.y_poolT[:, g, :], self.ps[:, b2, :], self.pscale[:, l, g:g + 1], ALU.mult)

    def ssd_branch(self, i, l):
        W = self.W
        for half in range(2):
            (wz,) = self.wload([self.win[:, :, O_Z + half * 512:O_Z + (half + 1) * 512]])
            for j in range(4):
                c = half * 4 + j
                b = self.bank("A")
                for k in range(8):
                    self.mm(self.ps[:, b, :], wz[:, k, j * 128:(j + 1) * 128], self.xnT[:, k, :], start=(k == 0), stop=(k == 7))
                self.act(self.zs[:, c, :], self.ps[:, b, :], AF.Silu)
        for third in range(3):
            (wx,) = self.wload([self.win[:, :, O_XBC + third * 512:O_XBC + (third + 1) * 512]])
            for j in range(4):
                c = third * 4 + j
                b = self.bank("A")
                for k in range(8):
                    self.mm(self.ps[:, b, :], wx[:, k, j * 128:(j + 1) * 128], self.xnT[:, k, :], start=(k == 0), stop=(k == 7))
                psb = self.ps[:, b, :]
                acc = self.cacc[self.rot("cacc", 2)]
                cw = lambda kk: self.convw[:, l, kk, c:c + 1]
                hal = self.chalo[:, l, c, :]
                self.act(acc[:], psb, AF.Identity, scale=cw(3), bias=self.convb[:, l, c:c + 1])
                self.stt(acc[:, 1:512], psb[:, 0:511], cw(2), acc[:, 1:512], ALU.mult, ALU.add)
                self.stt(acc[:, 2:512], psb[:, 0:510], cw(1), acc[:, 2:512], ALU.mult, ALU.add)
                self.stt(acc[:, 3:512], psb[:, 0:509], cw(0), acc[:, 3:512], ALU.mult, ALU.add)
                self.stt(acc[:, 0:1], hal[:, 2:3], cw(2), acc[:, 0:1], ALU.mult, ALU.add)
                self.stt(acc[:, 0:2], hal[:, 1:3], cw(1), acc[:, 0:2], ALU.mult, ALU.add)
                self.stt(acc[:, 0:3], hal[:, 0:3], cw(0), acc[:, 0:3], ALU.mult, ALU.add)
                self.copy(hal, psb[:, 509:512])
                if c < 8:
                    dst = self.xsT[:, c, :]
                elif c < 10:
                    dst = self.BT[:, c - 8, :]
                else:
                    dst = self.CT[:, c - 10, :]
                self.act(dst, acc[:], AF.Silu)
        (wdt,) = self.wload([self.win[:, :, O_DT:O_DT + 16]])
        b = self.bank("A")
        for k in range(8):
            self.mm(self.ps[0:16, b, :], wdt[:, k, :], self.xnT[:, k, :], start=(k == 0), stop=(k == 7))
        dtT, daT, acsT, decT, eaT = (t[0:16, :] for t in (self.dtT, self.daT, self.acsT, self.decT, self.eaT))
        self.act(dtT, self.ps[0:16, b, :], AF.Exp, bias=self.dtb[:, l:l + 1])
        self.act(dtT, dtT, AF.Ln, bias=self.one_t[0:16, :])
        self.ts(daT, dtT, self.negA[:, l:l + 1], ALU.mult)
        for ch in range(4):
            sl = slice(ch * 128, (ch + 1) * 128)
            o_, d0, d1 = acsT[:, sl], self.ones_f[0:16, :], daT[:, sl]
            self.op("dve", lambda e, o_=o_, d0=d0, d1=d1: e.tensor_tensor_scan(out=o_, data0=d0, data1=d1, initial=0.0, op0=ALU.mult, op1=ALU.add),
                    [d0, d1], [o_])
            self.act(decT[:, sl], acsT[:, sl], AF.Exp, scale=-1.0, bias=acsT[:, ch * 128 + 127:ch * 128 + 128])
        self.act(eaT, acsT, AF.Exp)
        Sf = self.S_f[:, l, :]
        self.copy(self.Sb[:], Sf)
        for ch in range(4):
            sl = slice(ch * 128, (ch + 1) * 128)
            b = self.bank("A")
            for q_, src in enumerate((dtT, daT, decT, eaT)):
                self.tr(self.ps[:, b, q_ * 16:(q_ + 1) * 16], src[:, sl], self.ident_f[0:16, 0:16])
            self.copy(self.tok[:], self.ps[:, b, 0:64])
            self.ts(self.dg[0:16, :], self.ident_f[0:16, 0:16], eaT[:, ch * 128 + 127:ch * 128 + 128], ALU.mult)
            b = self.bank("A")
            self.mm(self.ps[:, b, 0:16], self.ones_f[0:16, :], self.dg[0:16, :])
            self.copy(self.etot[:], self.ps[:, b, 0:16])
            for half in range(2):
                for c in range(4):
                    self.tr(self.ps[:, 4 + half, c * 128:(c + 1) * 128], self.xsT[:, half * 4 + c, sl], self.ident_f[:])
            xs_tok = self.ps[:, 4:6, :].rearrange("p b (h e) -> p (b h) e", e=64)
            self.tt(self.Xb[:], xs_tok, self.tok[:, 0:16].unsqueeze(2).broadcast_to([128, 16, 64]), ALU.mult)
            self.tt(self.Xd[:], self.Xb[:], self.tok[:, 32:48].unsqueeze(2).broadcast_to([128, 16, 64]), ALU.mult)
            b = self.bank("A")
            for g_ in range(2):
                self.tr(self.ps_b[:, b, g_ * 128:(g_ + 1) * 128], self.BT[:, g_, sl], self.ident_b[:])
            self.copy(self.Btok[:], self.ps_b[:, b, 0:256].rearrange("p (c n) -> p c n", c=2), eng="act")
            for g_ in range(2):
                b = self.bank("A")
                self.mm(self.ps[:, b, 0:128], self.BT[:, g_, sl], self.CT[:, g_, sl])
                self.tt(self.CBm[:, g_, :], self.ps[:, b, 0:128], self.tri_ml[:], ALU.mult)
            self.tt(self.A_all[:], self.mgt[:].unsqueeze(1).broadcast_to([128, 16, 128]),
                    self.tok[:, 16:32].unsqueeze(2).broadcast_to([128, 16, 128]), ALU.mult)
            for h in range(16):
                self.mm(self.ps[:, h // 4, (h % 4) * 128:(h % 4 + 1) * 128], self.A_all[:, h, :], self.tri_ml[:], skip_group_check=True)
            for bk in range(4):
                self.act(self.Mh_all[:, 4 * bk:4 * bk + 4, :], self.ps[:, bk, :].rearrange("p (h n) -> p h n", h=4), AF.Exp)
            for bk in range(4):
                self.tt(self.Mh_all[:, 4 * bk:4 * bk + 4, :], self.Mh_all[:, 4 * bk:4 * bk + 4, :],
                        self.CBm[:, bk // 2, :].unsqueeze(1).broadcast_to([128, 4, 128]), ALU.mult)
            for h in range(16):
                self.mm(self.ps[:, 6 + h // 8, (h % 8) * 64:(h % 8 + 1) * 64], self.Mh_all[:, h, :], self.Xb[:, h, :],
                        start=(h % 8 == 0), stop=True, skip_group_check=True)
            for g_ in range(2):
                self.mm(self.ps[:, 4 + g_, :], self.CT[:, g_, sl], self.Sb[:, g_ * 512:(g_ + 1) * 512])
            yo = self.ps[:, 4:6, :].rearrange("p b (h e) -> p (b h) e", e=64)
            self.tt(self.yt1[:], yo, self.tok[:, 48:64].unsqueeze(2).broadcast_to([128, 16, 64]), ALU.mult)
            yd = self.ps[:, 6:8, :].rearrange("p b n -> p (b n)")
            self.tt(self.ytok[:], self.yt1[:].rearrange("p h e -> p (h e)"), yd, ALU.add)
            for g_ in range(2):
                self.mm(self.ps[:, 6 + g_, :], self.Btok[:, g_, :], self.Xd[:, g_ * 8:(g_ + 1) * 8, :].rearrange("p h e -> p (h e)"))
            Sf3 = Sf.rearrange("p (h e) -> p h e", e=64)
            self.tt(Sf3, Sf3, self.etot[:].unsqueeze(2).broadcast_to([128, 16, 64]), ALU.mult)
            self.tt(Sf, Sf, self.ps[:, 6:8, :].rearrange("p b n -> p (b n)"), ALU.add)
            self.copy(self.Sb[:], Sf)
            for half in range(2):
                for c in range(4):
                    cc = half * 4 + c
                    self.tr(self.ps[:, 4 + half, c * 128:(c + 1) * 128], self.ytok[:, cc * 128:(cc + 1) * 128], self.ident_f[:])
                for c in range(4):
                    cc = half * 4 + c
                    xc = self.xsT[:, cc, sl]
                    self.stt(xc, xc, self.dexp[:, l, cc:cc + 1], self.ps[:, 4 + half, c * 128:(c + 1) * 128], ALU.mult, ALU.add)
        for c in range(8):
            self.tt(self.xsT[:, c, :], self.xsT[:, c, :], self.zs[:, c, :], ALU.mult)
        self.rmsnorm_T(self.xsT[:, :, :], 8, lambda c: self.g_ssd[:, l, c:c + 1], self.zs[:, :, :], D)

    def att_branch(self, i, l):
        W = self.W
        win = self.win
        t0 = i * TT
        (wqi,) = self.wload([win[:, :, O_QI:O_QI + 512]])
        for c in range(4):
            b = self.bank("A")
            for k in range(8):
                self.mm(self.ps[:, b, :], wqi[:, k, c * 128:(c + 1) * 128], self.xnT[:, k, :], start=(k == 0), stop=(k == 7))
            self.copy(self.qiT[:, c, :], self.ps[:, b, :], eng="act" if c % 2 else "dve")
        s_ = self.wslot()
        slot = self.wbuf[s_]
        wc = slot[:, 0:2048].rearrange("p (c n) -> p c n", c=8)
        wk = slot[:, 2048:3072].rearrange("p (c n) -> p c n", c=8)
        wi = slot[:, 3072:3136].rearrange("p (c n) -> p c n", c=8)
        grp = "w%d" % s_
        self.dma("pool", wc, win[:, :, O_CKV:O_CKV + 256], [], [wc], grp)
        self.dma("pool", wk[:, :, 0:64], win[:, :, O_KI:O_KI + 64], [], [wk[:, :, 0:64]], grp)
        self.dma("pool", wk[:, :, 64:128], win[:, :, O_KI:O_KI + 64], [], [wk[:, :, 64:128]], grp)
        self.dma("pool", wi, win[:, :, O_WI:O_WI + 8], [], [wi], grp)
        b = self.bank("A")
        for k in range(8):
            self.mm(self.ps[:, b, :], wk[:, k, :], self.xnT[:, k, :], start=(k == 0), stop=(k == 7))
        kf = self.f32t[self.rot("f32t", 2)]
        self.copy(kf[:], self.ps[:, b, :])
        sq = self.sq[self.rot("sq", 2)]
        self.act(sq[:], kf[:], AF.Square)
        b = self.bank("A")
        self.mm(self.ps[:, b, :], self.ones_b[0:64, :], sq[0:64, :])
        self.act(self.rstd[:], self.ps[:, b, :], AF.Sqrt, bias=self.eps_t[:, 0:1], scale=1.0 / 64)
        self.op("dve", lambda e: e.reciprocal(out=self.rstd[:], in_=self.rstd[:]), [self.rstd[:]], [self.rstd[:]])
        self.stt(self.kin[:, 0, :], kf[:], self.g_ki[:, l:l + 1], self.rstd[:], ALU.mult, ALU.mult)
        self.dma("sp", self.ic[l, :, t0:t0 + TT], self.kin[:, 0, :], [self.kin[:, 0, :]], [("ic", l, i)], "cw_ic")
        b = self.bank("A")
        for k in range(8):
            self.mm(self.ps[0:8, b, :], wi[:, k, :], self.xnT[:, k, :], start=(k == 0), stop=(k == 7))
        self.copy(self.wiT[:], self.ps[0:8, b, :])
        b = self.bank("A")
        for jj in range(4):
            self.tr(self.ps[:, b, jj * 8:(jj + 1) * 8], self.wiT[:, jj * 128:(jj + 1) * 128], self.ident_f[0:8, 0:8])
        self.copy(self.w_tok[:], self.ps[:, b, 0:32].rearrange("p (j h) -> p j h", h=8))
        nk_all = (i + 1) * TT
        self.dma("sp", self.kidx_sb[:, 0:nk_all], self.ic[l, :, 0:nk_all], [("ic", l, t) for t in range(i + 1)], [self.kidx_sb[:, 0:nk_all]], "kidx")
        kcv = self.kc[l].rearrange("(c p) s -> p c s", p=128)

        def gen_proj():
            (wq,) = self.wload([win[:, :, O_Q:O_Q + 512]])
            for c in range(4):
                b = self.bank("A")
                for k in range(8):
                    self.mm(self.ps[:, b, :], wq[:, k, c * 128:(c + 1) * 128], self.xnT[:, k, :], start=(k == 0), stop=(k == 7))
                self.copy(self.qT[:, c, :], self.ps[:, b, :], eng="act" if c % 2 else "dve")
                yield
            for c in range(2):
                b = self.bank("A")
                for k in range(8):
                    self.mm(self.ps[:, b, :], wc[:, k, c * 128:(c + 1) * 128], self.xnT[:, k, :], start=(k == 0), stop=(k == 7))
                self.copy(self.ckv_f[:, c, :], self.ps[:, b, :], eng="act")
            yield
            self.rmsnorm_T(self.ckv_f[:, :, :], 2, lambda c: self.g_kv[:, l, c:c + 1], self.ckvn[:, :, :], 256)
            yield
            wuk_v = W["w_uk"][l].rearrange("(c p) n -> p c n", p=128)
            wuv_v = W["w_uv"][l].rearrange("(c p) n -> p c n", p=128)
            wuk, wuv = self.wload([wuk_v, wuv_v])
            for c in range(4):
                b = self.bank("A")
                for k in range(2):
                    self.mm(self.ps[:, b, :], wuk[:, k, c * 128:(c + 1) * 128], self.ckvn[:, k, :], start=(k == 0), stop=(k == 1))
                self.copy(self.kt_out[:, c, :], self.ps[:, b, :], eng="act" if c % 2 else "dve")
                yield
            self.dma("sp", self.kc[l].rearrange("(c p) s -> p c s", p=128)[:, :, t0:t0 + TT], self.kt_out[:], [self.kt_out[:]], [("kc", l, i)], "cw_kc")
            v4 = self.v_out[:].rearrange("p c (h e) -> p c h e", e=72)
            for sbk in range(4):
                self.memset(v4[:, sbk, :, 64:65], 1.0)
                b = self.bank("A")
                for k in range(2):
                    self.mm(self.ps[:, b, :], self.ckvn[:, k, sbk * 128:(sbk + 1) * 128], wuv[:, k, :], start=(k == 0), stop=(k == 1))
                self.copy(v4[:, sbk, :, 0:64], self.ps[:, b, :].rearrange("p (h e) -> p h e", e=64), eng="act" if sbk % 2 else "dve")
                yield
            self.dma("sp", self.vc[l, t0:t0 + TT, :].rearrange("(c p) e -> p c e", p=128), self.v_out[:], [self.v_out[:]], [("vc", l, i)], "cw_vc")

        def pre(jj):
            jg = 4 * i + jj
            n = 128 * (jg + 1)
            tq = slice(jj * 128, (jj + 1) * 128)
            acc = self.idx_acc
            for h in range(8):
                for c0 in range(0, n, 512):
                    N = min(512, n - c0)
                    b = self.bank("A")
                    po = (h % 2) * 64
                    self.mm(self.ps[:, b, 0:N], self.qiT[po:po + 64, h // 2, tq], self.kidx_sb[po:po + 64, c0:c0 + N])
                    tmp = (self.rtmp + self.f32t)[self.rot("rtmp", 4)]
                    self.act(tmp[:, 0:N], self.ps[:, b, 0:N], AF.Relu)
                    if h == 0:
                        self.ts(acc[:, c0:c0 + N], tmp[:, 0:N], self.w_tok[:, jj, 0:1], ALU.mult)
                    else:
                        self.stt(acc[:, c0:c0 + N], tmp[:, 0:N], self.w_tok[:, jj, h:h + 1], acc[:, c0:c0 + N], ALU.mult, ALU.add)
            self.tt(acc[:, n - 128:n], acc[:, n - 128:n], self.neg_tri[:], ALU.add)

        def gen_bis(jj):
            jg = 4 * i + jj
            n = 128 * (jg + 1)
            acc = self.idx_acc
            if jg >= 2:
                yield from self.bisect(acc, n)
            else:
                self.memset(self.thr[:], -1.0e29)
            self.ts(self.masks[jj % 2][:, 0:n], acc[:, 0:n], self.thr[:, 0:1], ALU.is_lt, s2=-30000.0, op1=ALU.mult)
            yield

        def gen_p2(jj):
            jg = 4 * i + jj
            n = 128 * (jg + 1)
            tq = slice(jj * 128, (jj + 1) * 128)
            mask_ = self.masks[jj % 2]
            nsb = jg + 1
            slot_of = {}

            for h in range(8):
                po = (h % 2) * 64
                self.copy(self.qz[po:po + 64, h, :], self.qT[po:po + 64, h // 2, tq], eng="act" if h % 2 else "dve")

            def emit_TS(k):
                c, sbk = k // 4, k % 4
                if sbk == 0:
                    c0 = c * 512
                    nb = min(4, nsb - c * 4)
                    sl_ = self.rot("kvst", 2)
                    slot_of[c] = sl_
                    kst, vst = self.kst[sl_], self.vst[sl_]
                    self.dma("sp", kst[:, :, 0:nb * 128], kcv[:, :, c0:c0 + nb * 128], [("kc", l, c)], [kst[:, :, 0:nb * 128]], "kv%d" % sl_)
                    self.dma("sp", vst[:, 0:nb, :], self.vc[l, c0:c0 + nb * 128, :].rearrange("(c p) e -> p c e", p=128), [("vc", l, c)],
                             [vst[:, 0:nb, :]], "kv%d" % sl_)
                kst = self.kst[slot_of[c]]
                sb0 = 2 + 2 * (k % 2)
                for h in range(8):
                    reg = self.ps[:, sb0 + h // 4, (h % 4) * 128:(h % 4 + 1) * 128]
                    self.mm(reg, kst[:, h // 2, sbk * 128:(sbk + 1) * 128], self.qz[:, h, :], start=True, stop=False, skip_group_check=True)
                    self.mm(reg, mask_[:, k * 128:(k + 1) * 128], self.ident_b[:], start=False, stop=True, skip_group_check=True)

            def emit_rest(k):
                c, sbk = k // 4, k % 4
                vst = self.vst[slot_of[c]]
                sb0 = 2 + 2 * (k % 2)
                pT = self.pTt[k % 2]
                self.act(pT[:], self.ps[:, sb0:sb0 + 2, :].rearrange("p b (h t) -> p (b h) t", t=128), AF.Exp, scale=0.125)
                for h in range(8):
                    self.mm(self.ps[:, 6 + h // 4, (h % 4) * 72:(h % 4) * 72 + 65], pT[:, h, :], vst[:, sbk, h * 72:h * 72 + 65],
                            start=(k == 0 and h % 4 == 0), stop=(k == nsb - 1), skip_group_check=True)

            emit_TS(0)
            for k in range(nsb):
                if k + 1 < nsb:
                    emit_TS(k + 1)
                emit_rest(k)
                yield
            ov = self.ps[:, 6:8, 0:288].rearrange("p b (h e) -> p b h e", e=72)
            rc = self.rcp[:].rearrange("p (b h) -> p b h", b=2)
            for bb_ in range(2):
                den = ov[:, bb_, :, 64:65].rearrange("p h e -> p (h e)")
                self.op("dve", lambda e, o_=rc[:, bb_, :], d_=den: e.reciprocal(out=o_, in_=d_), [den], [rc[:, bb_, :]])
            o3 = self.o_sb[:].rearrange("p (h e) -> p h e", e=64)
            for h in range(8):
                self.ts(o3[:, h, :], ov[:, h // 4, h % 4, 0:64], self.rcp[:, h:h + 1], ALU.mult)
            b = self.bank("A")
            for c in range(4):
                self.tr(self.ps_b[:, b, c * 128:(c + 1) * 128], self.o_sb[:, c * 128:(c + 1) * 128], self.ident_b[:])
            self.copy(self.y_attnT[:, :, tq], self.ps_b[:, b, 0:512].rearrange("p (c n) -> p c n", c=4), eng="act")

            yield

        def drain(g):
            for _ in g:
                pass

        def interleave(ga, na, gb, nb_):
            da = db = False
            ia = ib = 0
            while not (da and db):
                if not da and (db or ia * max(nb_, 1) <= ib * max(na, 1)):
                    try:
                        next(ga)
                        ia += 1
                    except StopIteration:
                        da = True
                elif not db:
                    try:
                        next(gb)
                        ib += 1
                    except StopIteration:
                        db = True

        self.memset(self.qz[:], 0.0)

        def g1():
            pre(0)
            yield
            yield from gen_bis(0)

        interleave(g1(), NBIS + 2, gen_proj(), 15)
        for jj in range(4):
            if jj + 1 < 4:
                pre(jj + 1)
                interleave(gen_bis(jj + 1), NBIS + 1, gen_p2(jj), 4 * i + jj + 2)
            else:
                drain(gen_p2(jj))

    def bisect(self, acc, n):
        lo, hi, zA, c_, t1, cntB, d_, z_ = (self.bis[:, k, 0:1] for k in range(8))
        junk = self.junk
        split = n >= 1536
        nA = max(256, (n * 52 // 100) // 256 * 256) if split else n
        nB = n - nA
        theta = 255.5 - nB / 2.0
        self.op("dve", lambda e: e.tensor_reduce(out=lo, in_=acc[:, 0:n - 128], axis=AX.X, op=ALU.min), [acc[:, 0:n - 128]], [lo])
        self.op("dve", lambda e: e.tensor_reduce(out=hi, in_=acc[:, 0:n], axis=AX.X, op=ALU.max), [acc[:, 0:n]], [hi])
        self.tt(t1, hi, lo, ALU.subtract)
        self.ts(self.hks[:], self.pw2[:], t1, ALU.mult, s2=0.5, op1=ALU.mult)
        self.tt(c_, lo, self.hks[:, 0:1], ALU.add)
        for k in range(NBIS):
            hk = self.hks[:, k:k + 1]
            self.ts(junk[:, 0:nA], acc[:, 0:nA], c_, ALU.is_ge, s2=-theta, op1=ALU.add, accum_out=zA)
            if split:
                self.act(junk[:, nA:n], acc[:, nA:n], AF.Sign, bias=c_, scale=-1.0, accum_out=cntB)
            self.stt(d_, hk, -0.5, c_, ALU.mult, ALU.add)
            if split:
                self.stt(z_, cntB, -0.5, zA, ALU.mult, ALU.add)
                zz = z_
            else:
                zz = zA
            self.ts(t1, zz, 0.0, ALU.is_ge, s2=hk, op1=ALU.mult)
            self.tt(c_, t1, d_, ALU.add)
            yield
        self.tt(self.thr[:], c_, self.hks[:, NBIS:NBIS + 1], ALU.subtract)

    def ple(self, i, l):
        W = self.W
        g = self.g_ple
        self.rmsnorm_T(self.hT[:, :, :], 8, lambda c: g[:, l, c:c + 1], self.xnT[:, :, :], D)
        for j in range(4):
            xi_ = self.rot("xin", 2)
            xb = self.xin[xi_]
            r0 = i * TT + j * 128
            self.dma("sp", xb[:, 0:256], self.p[l, r0:r0 + 128, :], [], [xb[:, 0:256]], "xin%d" % xi_)
            b = self.bank("A")
            for c in range(2):
                self.tr(self.ps[:, b, c * 128:(c + 1) * 128], xb[:, c * 128:(c + 1) * 128], self.ident_f[:])
            self.copy(self.pT[:, :, j * 128:(j + 1) * 128], self.ps[:, b, 0:256].rearrange("p (c n) -> p c n", c=2), eng="act")
        wg_v = W["ple_w_gate"][l].rearrange("(c p) n -> p c n", p=128)
        wp_v = W["ple_w_proj"][l].rearrange("(c p) n -> p c n", p=128)
        for half in range(2):
            (wg,) = self.wload([wg_v[:, :, half * 512:(half + 1) * 512]])
            (wp,) = self.wload([wp_v[:, :, half * 512:(half + 1) * 512]])
            for dc in range(4):
                bg = self.bank("A")
                bp = self.bank("A")
                for k in range(8):
                    self.mm(self.ps[:, bg, :], wg[:, k, dc * 128:(dc + 1) * 128], self.xnT[:, k, :], start=(k == 0), stop=(k == 7))
                for k in range(2):
                    self.mm(self.ps[:, bp, :], wp[:, k, dc * 128:(dc + 1) * 128], self.pT[:, k, :], start=(k == 0), stop=(k == 1))
                self.act(self.pleg[:], self.ps[:, bg, :], AF.Sigmoid)
                tmp = self.f32t[self.rot("f32t", 2)]
                self.tt(tmp[:], self.pleg[:], self.ps[:, bp, :], ALU.mult)
                hc = self.hT[:, half * 4 + dc, :]
                self.tt(hc, hc, tmp[:], ALU.add)

    def final(self, i):
        b = self.bank("A")
        acc = self.ps[:, b, :]
        for c in range(8):
            sq = self.sq[self.rot("sq", 2)]
            self.act(sq[:], self.hT[:, c, :], AF.Square)
            self.mm(acc, self.ones_b[:], sq[:], start=(c == 0), stop=(c == 7))
        self.act(self.rstd[:], acc, AF.Sqrt, bias=self.eps_t[:, 0:1], scale=1.0 / D)
        self.op("dve", lambda e: e.reciprocal(out=self.rstd[:], in_=self.rstd[:]), [self.rstd[:]], [self.rstd[:]])
        for c in range(8):
            self.stt(self.hT[:, c, :], self.hT[:, c, :], self.g_fin[:, c:c + 1], self.rstd[:], ALU.mult, ALU.mult)
        for j in range(4):
            xi_ = self.rot("xin", 2)
            xb = self.xin[xi_]
            for half in range(2):
                b = self.bank("A")
                for c in range(4):
                    cc = half * 4 + c
                    self.tr(self.ps[:, b, c * 128:(c + 1) * 128], self.hT[:, cc, j * 128:(j + 1) * 128], self.ident_f[:])
                self.copy(xb[:, half * 512:(half + 1) * 512], self.ps[:, b, :], eng="act" if half else "dve")
            r0 = i * TT + j * 128
            self.dma("sp", self.out[r0:r0 + 128, :], xb[:], [xb[:]], [("out", r0)], "out")


_CACHE = {}


def get_nc(NT, NL, mix=7):
    key = (NT, NL, mix)
    if key not in _CACHE:
        _CACHE[key] = Builder(NT, NL, mix).build()
    return _CACHE[key]


W_NAMES = ["ffn1_norm", "ffn1_w_gate", "ffn1_w_up", "ffn1_w_down", "mix_norm", "w_in", "kv_norm", "idx_k_norm",
           "w_uk", "w_uv", "pool_w", "pool_scale", "conv_w", "conv_b", "dt_bias", "a_log", "d_skip", "ssd_norm",
           "w_br_attn", "w_br_pool", "w_br_ssd", "w_out", "ffn2_norm", "ffn2_w_gate", "ffn2_w_up", "ffn2_w_down",
           "ple_norm", "ple_w_gate", "ple_w_proj", "final_norm"]


def run(inputs, NT, NL, ncores=8, mix=7):
    nc = get_nc(NT, NL, mix)
    S = NT * TT
    shared = {}
    for n in W_NAMES:
        a = np.ascontiguousarray(np.asarray(inputs[n], dtype=np.float32))
        if n in ("w_uk", "w_uv"):
            a = a.reshape(L_FULL, 256, 512)
        shared[n] = a
    x = np.asarray(inputs["x"], dtype=np.float32)
    p = np.asarray(inputs["p"], dtype=np.float32)
    in_maps = []
    for b in range(ncores):
        m = dict(shared)
        m["x"] = np.ascontiguousarray(x[b, :S])
        m["p"] = np.ascontiguousarray(p[:, b, :S])
        in_maps.append(m)
    res = run_bass_kernel_spmd(nc, in_maps, core_ids=list(range(ncores)))
    return np.stack([np.asarray(r["out"]) for r in res.results], axis=0)


def kernel(**inputs):
    return run(inputs, S_FULL // TT, L_FULL, 8).astype(np.float32)
```
